# Optimizing a Trainium2 kernel written in Bass

```python
import jax, jax.numpy as jnp
from jax import lax
import numpy as np

D_MODEL = 1024
BATCH = 4
SEQ = 8192
DEPTH = 1

D_RNN = 1024
RNN_BLOCKS = 8
RNN_BW = D_RNN // RNN_BLOCKS
CONV_W = 4
LRU_C = 8.0
N_HEADS = 8
HEAD_DIM = 128
D_ATTN = N_HEADS * HEAD_DIM
MOBA_BLOCK = 256
MOBA_TOPK = 3
Q_CHUNK = 32
N_BRANCH = 2
D_FF = 2816
FFN_CONV_W = 3

D_IN = 2 * D_RNN + 3 * D_ATTN + N_BRANCH * D_MODEL
EPS = 1e-6
NEG = -1e30

kernel_name = "hybrid_rglru_moba_convffn_block"


def rmsnorm(x, g):
    xf = x.astype(jnp.float32)
    y = xf * lax.rsqrt(jnp.mean(xf * xf, axis=-1, keepdims=True) + EPS)
    return (y * g.astype(jnp.float32)).astype(x.dtype)


def causal_dwconv(x, w, b):
    width = w.shape[0]
    s = x.shape[1]
    xp = jnp.pad(x, ((0, 0), (width - 1, 0), (0, 0)))
    y = b
    for k in range(width):
        y = y + xp[:, k:k + s] * w[k]
    return y


def rg_lru(x, w_r, b_r, w_i, b_i, lam):
    bsz, s, _ = x.shape
    xb = x.reshape(bsz, s, RNN_BLOCKS, RNN_BW)
    r = jax.nn.sigmoid(jnp.einsum('bsnc,ncd->bsnd', xb, w_r) + b_r).reshape(bsz, s, D_RNN)
    i = jax.nn.sigmoid(jnp.einsum('bsnc,ncd->bsnd', xb, w_i) + b_i).reshape(bsz, s, D_RNN)
    log_a = (-LRU_C * r.astype(jnp.float32)) * jax.nn.softplus(-lam.astype(jnp.float32))
    a = jnp.exp(log_a)
    mult = jnp.sqrt(-jnp.expm1(2.0 * log_a))
    bx = mult * (i * x).astype(jnp.float32)

    def combine(c1, c2):
        a1, b1 = c1
        a2, b2 = c2
        return a1 * a2, a2 * b1 + b2

    _, h = lax.associative_scan(combine, (a, bx), axis=1)
    return h.astype(x.dtype)


def moba_attention(q, k, v):
    bsz, nh, s, hd = q.shape
    nb = -(-s // MOBA_BLOCK)
    s_pad = nb * MOBA_BLOCK
    pad = ((0, 0), (0, 0), (0, s_pad - s), (0, 0))
    q = jnp.pad(q, pad)
    k = jnp.pad(k, pad)
    v = jnp.pad(v, pad)
    kb = k.reshape(bsz, nh, nb, MOBA_BLOCK, hd)
    vb = v.reshape(bsz, nh, nb, MOBA_BLOCK, hd)
    k_mean = jnp.mean(kb.astype(jnp.float32), axis=3)
    n_sel = min(MOBA_TOPK, nb)
    n_chunks = s_pad // Q_CHUNK
    scale = HEAD_DIM ** -0.5
    qc = q.reshape(bsz, nh, n_chunks, Q_CHUNK, hd).transpose(2, 0, 1, 3, 4)
    gather = jax.vmap(jax.vmap(lambda t, ix: t[ix]))

    def chunk_fn(args):
        ci, qi = args
        q_pos = ci * Q_CHUNK + jnp.arange(Q_CHUNK)
        own = (ci * Q_CHUNK) // MOBA_BLOCK
        gate = jnp.einsum('bhqd,bhnd->bhqn', qi.astype(jnp.float32), k_mean)
        gate = jnp.where(jnp.arange(nb) < own, gate, NEG)
        _, idx = lax.top_k(gate, n_sel)
        slot_ok = jnp.arange(n_sel) < own
        k_sel = gather(kb, idx)
        v_sel = gather(vb, idx)
        k_own = lax.dynamic_slice_in_dim(k, own * MOBA_BLOCK, MOBA_BLOCK, axis=2)
        v_own = lax.dynamic_slice_in_dim(v, own * MOBA_BLOCK, MOBA_BLOCK, axis=2)
        s_sel = jnp.einsum('bhqd,bhqnkd->bhqnk', qi, k_sel).astype(jnp.float32) * scale
        s_sel = jnp.where(slot_ok[:, None], s_sel, NEG).reshape(bsz, nh, Q_CHUNK, n_sel * MOBA_BLOCK)
        s_own = jnp.einsum('bhqd,bhkd->bhqk', qi, k_own).astype(jnp.float32) * scale
        k_pos = own * MOBA_BLOCK + jnp.arange(MOBA_BLOCK)
        s_own = jnp.where(k_pos[None, :] <= q_pos[:, None], s_own, NEG)
        p = jax.nn.softmax(jnp.concatenate([s_sel, s_own], axis=-1), axis=-1).astype(v.dtype)
        p_sel = p[..., :n_sel * MOBA_BLOCK].reshape(bsz, nh, Q_CHUNK, n_sel, MOBA_BLOCK)
        p_own = p[..., n_sel * MOBA_BLOCK:]
        return (jnp.einsum('bhqnk,bhqnkd->bhqd', p_sel, v_sel)
                + jnp.einsum('bhqk,bhkd->bhqd', p_own, v_own))

    o = lax.map(chunk_fn, (jnp.arange(n_chunks), qc))
    o = o.transpose(1, 2, 0, 3, 4).reshape(bsz, nh, s_pad, hd)[:, :, :s]
    return o


def setup_inputs(seed: int = 0) -> dict:
    key = jax.random.key(seed)
    ks = jax.random.split(key, 24)
    f32 = jnp.float32
    L = DEPTH

    def nrm(k, shape, scale):
        return jax.random.normal(k, shape, f32) * scale

    a0 = jax.random.uniform(ks[9], (L, D_RNN), f32, 0.9, 0.999)
    s0 = a0 ** (1.0 / LRU_C)
    lru_lambda = jnp.log(s0) - jnp.log1p(-s0)
    return {
        "x": jax.random.normal(ks[0], (BATCH, SEQ, D_MODEL), f32),
        "norm1_g": 1.0 + nrm(ks[1], (L, D_MODEL), 0.02),
        "w_in": nrm(ks[2], (L, D_MODEL, D_IN), D_MODEL ** -0.5),
        "conv_w": nrm(ks[3], (L, CONV_W, D_RNN), CONV_W ** -0.5),
        "conv_b": nrm(ks[4], (L, D_RNN), 0.01),
        "w_r": nrm(ks[5], (L, RNN_BLOCKS, RNN_BW, RNN_BW), RNN_BW ** -0.5),
        "b_r": nrm(ks[6], (L, RNN_BLOCKS, RNN_BW), 0.01),
        "w_i": nrm(ks[7], (L, RNN_BLOCKS, RNN_BW, RNN_BW), RNN_BW ** -0.5),
        "b_i": nrm(ks[8], (L, RNN_BLOCKS, RNN_BW), 0.01),
        "lru_lambda": lru_lambda,
        "q_norm_g": 1.0 + nrm(ks[10], (L, HEAD_DIM), 0.02),
        "k_norm_g": 1.0 + nrm(ks[11], (L, HEAD_DIM), 0.02),
        "w_proj_rnn": nrm(ks[12], (L, D_RNN, D_MODEL), D_RNN ** -0.5),
        "w_proj_attn": nrm(ks[13], (L, D_ATTN, D_MODEL), D_ATTN ** -0.5),
        "w_out": nrm(ks[14], (L, D_MODEL, D_MODEL), D_MODEL ** -0.5),
        "norm2_g": 1.0 + nrm(ks[15], (L, D_MODEL), 0.02),
        "w_up": nrm(ks[16], (L, D_MODEL, D_FF), D_MODEL ** -0.5),
        "w_gate": nrm(ks[17], (L, D_MODEL, D_FF), D_MODEL ** -0.5),
        "ffn_conv_w": nrm(ks[18], (L, FFN_CONV_W, D_FF), FFN_CONV_W ** -0.5),
        "ffn_conv_b": nrm(ks[19], (L, D_FF), 0.01),
        "w_down": nrm(ks[20], (L, D_FF, D_MODEL), D_FF ** -0.5),
    }


def reference(x, norm1_g, w_in, conv_w, conv_b, w_r, b_r, w_i, b_i, lru_lambda,
              q_norm_g, k_norm_g, w_proj_rnn, w_proj_attn, w_out, norm2_g,
              w_up, w_gate, ffn_conv_w, ffn_conv_b, w_down):
    bsz, s, _ = x.shape
    cuts = np.cumsum([D_RNN, D_RNN, D_ATTN, D_ATTN, D_ATTN])
    for l in range(DEPTH):
        h = rmsnorm(x, norm1_g[l])
        u = jnp.einsum('bsd,de->bse', h, w_in[l])
        x_rnn, g_rnn, q, k, v, g_br = jnp.split(u, list(cuts), axis=-1)
        xa = causal_dwconv(x_rnn, conv_w[l], conv_b[l])
        ya = rg_lru(xa, w_r[l], b_r[l], w_i[l], b_i[l], lru_lambda[l]) * jax.nn.gelu(g_rnn)
        q = rmsnorm(q.reshape(bsz, s, N_HEADS, HEAD_DIM), q_norm_g[l]).transpose(0, 2, 1, 3)
        k = rmsnorm(k.reshape(bsz, s, N_HEADS, HEAD_DIM), k_norm_g[l]).transpose(0, 2, 1, 3)
        v = v.reshape(bsz, s, N_HEADS, HEAD_DIM).transpose(0, 2, 1, 3)
        yb = moba_attention(q, k, v).transpose(0, 2, 1, 3).reshape(bsz, s, D_ATTN)
        gates = jax.nn.sigmoid(g_br).reshape(bsz, s, N_BRANCH, D_MODEL)
        merged = (gates[:, :, 0] * jnp.einsum('bsc,cd->bsd', ya, w_proj_rnn[l])
                  + gates[:, :, 1] * jnp.einsum('bsc,cd->bsd', yb, w_proj_attn[l]))
        x = x + jnp.einsum('bsd,de->bse', merged, w_out[l])
        h = rmsnorm(x, norm2_g[l])
        up = causal_dwconv(jnp.einsum('bsd,df->bsf', h, w_up[l]), ffn_conv_w[l], ffn_conv_b[l])
        act = jax.nn.gelu(up) * jnp.einsum('bsd,df->bsf', h, w_gate[l])
        x = x + jnp.einsum('bsf,fd->bsd', act, w_down[l])
    return x
```

```python
import contextlib
import numpy as np
import concourse.bass as bass
import concourse.mybir as mybir
from concourse.bass_utils import run_bass_kernel_spmd

F32 = mybir.dt.float32
BF16 = mybir.dt.bfloat16
ALU = mybir.AluOpType
AF = mybir.ActivationFunctionType
AX = mybir.AxisListType

PE, ACT, DVE, POOL, SP = "tensor", "scalar", "vector", "gpsimd", "sync"
COMPUTE = (PE, ACT, DVE, POOL)

D = 1024
S_LOC = 8192
OWN0 = 3968
NQ = 4224
DFF = 2816
NF = 22
BIG = 30000.0
EPS = 1e-6
SCALE = 128 ** -0.5


class Buf:
    __slots__ = ("name", "last_w", "readers", "excl")

    def __init__(self, name="", excl=False):
        self.name = name
        self.last_w = None
        self.readers = []
        self.excl = excl


class Op:
    __slots__ = ("eng", "fn", "deps", "users", "signal", "is_dma", "sem", "semval", "cost", "idx",
                 "nleft", "ready", "fin", "start", "seg", "pos", "deps_eff", "weak")

    def __init__(self, eng, fn, is_dma, cost):
        self.eng = eng
        self.fn = fn
        self.deps = []
        self.users = []
        self.signal = False
        self.is_dma = is_dma
        self.sem = None
        self.semval = None
        self.cost = cost
        self.start = 0.0
        self.fin = 0.0
        self.weak = set()


SYNC_LAT = 600.0


class Prog:
    def __init__(self, nc, n_dma_sems=32):
        self.nc = nc
        self.n_dma_sems = n_dma_sems
        self.segments = [[]]

    def add(self, eng, fn, reads=(), writes=(), is_dma=False, cost=500.0):
        op = Op(eng, fn, is_dma, cost)
        ex = [b for b in reads if b.excl and b not in writes]
        if ex:
            writes = list(writes) + ex
        strong = {}
        order = []

        def note(d, is_strong):
            if d is op:
                return
            k = id(d)
            if k not in strong:
                strong[k] = False
                order.append(d)
            if is_strong:
                strong[k] = True

        for b in reads:
            if b.last_w is not None:
                note(b.last_w, True)
        for b in writes:
            if b.last_w is not None:
                note(b.last_w, b.excl and b.last_w.eng != eng)
            for r in b.readers:
                note(r, True)
        for d in order:
            op.deps.append(d)
            if not strong[id(d)] and d.eng == eng and not d.is_dma and not is_dma:
                op.weak.add(id(d))
        for b in reads:
            b.readers.append(op)
        for b in writes:
            b.last_w = op
            b.readers = []
        op.seg = len(self.segments) - 1
        self.segments[-1].append(op)
        return op

    def dma(self, q, out, in_, reads=(), writes=(), nbytes=65536):
        return self.add(SP, lambda e: e.dma_start(out=out, in_=in_), reads, writes, is_dma=True,
                        cost=2000.0 + nbytes / 60.0)

    def barrier(self):
        self.segments.append([])

    def _schedule(self, ops, segi):
        import heapq
        engs = (PE, ACT, DVE, POOL, SP)
        for i, op in enumerate(ops):
            op.idx = i
            op.users = []
            op.nleft = 0
        for op in ops:
            for d in op.deps:
                if d.seg == segi:
                    d.users.append(op)
                    op.nleft += 1
        future = {e: [] for e in engs}
        avail = {e: [] for e in engs}
        free = {e: 0.0 for e in engs}
        for op in ops:
            if op.nleft == 0:
                op.ready = 0.0
                heapq.heappush(future[op.eng], (0.0, op.idx))
        order = {e: [] for e in engs}
        n = 0
        total = len(ops)
        while n < total:
            best = None
            for e in engs:
                T = free[e]
                fu = future[e]
                while fu and fu[0][0] <= T:
                    _, ix = heapq.heappop(fu)
                    heapq.heappush(avail[e], ix)
                if avail[e]:
                    cand = (T, avail[e][0], e, True)
                elif fu:
                    cand = (fu[0][0], fu[0][1], e, False)
                else:
                    continue
                if best is None or cand[:2] < best[:2]:
                    best = cand
            assert best is not None, "scheduler stuck (cyclic deps?)"
            st_, ix, e, from_avail = best
            if from_avail:
                heapq.heappop(avail[e])
            else:
                heapq.heappop(future[e])
            op = ops[ix]
            op.start = st_
            if op.is_dma:
                free[e] = st_ + 70.0
            else:
                free[e] = st_ + op.cost
            op.fin = st_ + op.cost
            order[e].append(op)
            n += 1
            for u in op.users:
                u.nleft -= 1
                if u.nleft == 0:
                    r = 0.0
                    for d in u.deps:
                        if d.seg == segi:
                            lat = 0.0 if (d.eng == u.eng and not d.is_dma) else SYNC_LAT
                            r = max(r, d.fin + lat)
                    u.ready = r
                    heapq.heappush(future[u.eng], (r, u.idx))
        return order

    def emit(self):
        nc = self.nc
        engs = (PE, ACT, DVE, POOL, SP)
        streams = {e: [] for e in engs}
        glob = []
        tbase = 0.0
        for segi, ops in enumerate(self.segments):
            order = self._schedule(ops, segi)
            tmax = 0.0
            for op in ops:
                op.start += tbase
                op.fin += tbase
                tmax = max(tmax, op.fin)
            for e in engs:
                streams[e].extend(order[e])
            glob.extend(sorted(ops, key=lambda o: (o.start, o.idx)))
            if segi < len(self.segments) - 1:
                lasts = [order[e][-1] for e in (PE, ACT, DVE, POOL) if order[e] and not order[e][-1].is_dma]
                dmas = [o for o in ops if o.is_dma]
                for e in engs:
                    b = Op(e, lambda eng: eng.nop(), False, 50.0)
                    b.seg = segi
                    b.idx = len(ops)
                    b.deps = list(lasts) + dmas
                    b.start = tmax
                    b.fin = tmax + 50.0
                    streams[e].append(b)
                    glob.append(b)
            tbase = tmax + 100.0
        self.est_ns = tbase
        for e in engs:
            for i, op in enumerate(streams[e]):
                op.pos = i
        for e in engs:
            for op in streams[e]:
                best = {}
                eff = []
                for d in op.deps:
                    if d.is_dma:
                        eff.append(d)
                        continue
                    if d.eng == PE and op.eng == PE and not op.is_dma:
                        continue
                    if id(d) in op.weak:
                        continue
                    cur = best.get(d.eng)
                    if cur is None or d.pos > cur.pos:
                        best[d.eng] = d
                for d in best.values():
                    d.signal = True
                    eff.append(d)
                op.deps_eff = eff
        done = set()
        ptr = {e: 0 for e in engs}
        progress = True
        while progress:
            progress = False
            for e in engs:
                while ptr[e] < len(streams[e]):
                    op = streams[e][ptr[e]]
                    if all(id(d) in done for d in op.deps):
                        done.add(id(op))
                        ptr[e] += 1
                        progress = True
                    else:
                        break
        assert all(ptr[e] == len(streams[e]) for e in engs), "deadlock in scheduled streams"
        with contextlib.ExitStack() as es:
            esems = {e: es.enter_context(nc.semaphore("s_" + e)) for e in COMPUTE}
            dsems = [es.enter_context(nc.semaphore("d%d" % i)) for i in range(self.n_dma_sems)]
            for e in COMPUTE:
                c = 0
                for op in streams[e]:
                    if op.is_dma:
                        continue
                    if op.signal:
                        c += 1
                        op.sem = esems[e]
                        op.semval = c
            k = 0
            dma_prev = {}
            for op in streams[SP]:
                if op.is_dma:
                    op.signal = True
                    si = k % self.n_dma_sems
                    op.sem = dsems[si]
                    op.semval = 16 * (k // self.n_dma_sems + 1)
                    prev = dma_prev.get(si)
                    if prev is not None:
                        op.deps_eff.append(prev)
                    dma_prev[si] = op
                    k += 1
            finals = list(dma_prev.values())
            block = es.enter_context(nc.Block())

            def run_engine(eng_name):
                def body(eng):
                    waited = {}
                    for op in streams[eng_name]:
                        for d in op.deps_eff:
                            if d.sem is None:
                                continue
                            key = id(d.sem)
                            if waited.get(key, 0) >= d.semval:
                                continue
                            eng.wait_ge(d.sem, d.semval)
                            waited[key] = d.semval
                        ins = op.fn(eng)
                        if op.signal and op.sem is not None:
                            ins.then_inc(op.sem, 16 if op.is_dma else 1)
                    if eng_name == SP:
                        for d in finals:
                            if waited.get(id(d.sem), 0) < d.semval:
                                eng.wait_ge(d.sem, d.semval)
                return body

            block.sync(run_engine(SP))
            block.tensor(run_engine(PE))
            block.scalar(run_engine(ACT))
            block.vector(run_engine(DVE))
            block.gpsimd(run_engine(POOL))


class Tl:
    __slots__ = ("t", "b", "cb")

    def __init__(self, t, name, excl=False):
        self.t = t
        self.b = Buf(name, excl)
        self.cb = None


def build_program(debug=False):
    nc = bass.Bass("TRN2", target_bir_lowering=False)
    P = Prog(nc)

    def din(name, shape, dt=F32):
        return nc.dram_tensor(name, list(shape), dt, kind="ExternalInput").ap()

    def dscr(name, shape, dt):
        return nc.dram_tensor(name, list(shape), dt, kind=("ExternalOutput" if debug else "Internal")).ap()

    x_d = din("xl", [S_LOC, D])
    flag_d = din("flag", [128, 1])
    blkbias_d = din("blkbias", [128, 32])
    pastb_d = din("pastb", [128, 17 * 32])
    om1big_d = din("om1big", [128, 17 * 32])
    esel_d = din("esel", [32, 32 * 128])
    cmask_d = din("cmask", [128, 4 * 512])
    g1T_d = din("g1T", [128, 8])
    g2T_d = din("g2T", [128, 8])
    cw_d = din("cw", [128, 32])
    cb_d = din("cb", [128, 8])
    br_d = din("brT", [128, 8])
    bi_d = din("biT", [128, 8])
    lam_d = din("lamT", [128, 8])
    qg_d = din("qg", [1, 128])
    kg_d = din("kg", [1, 128])
    fcw_d = din("fcw", [128, 66])
    fcb_d = din("fcb", [128, 22])
    win_d = din("w_in", [D, 7168])
    wr_d = din("w_r", [8, 128, 128])
    wi_d = din("w_i", [8, 128, 128])
    pa_d = din("w_pa", [D, D])
    pb_d = din("w_pb", [D, D])
    wo_d = din("w_out", [D, D])
    wu_d = din("w_up", [D, DFF])
    wg_d = din("w_gate", [D, DFF])
    wd_d = din("w_down", [DFF, D])
    out_d = nc.dram_tensor("out", [4096, D], F32, kind="ExternalOutput").ap()

    KT_d = dscr("KT", [8, 128, S_LOC], BF16)
    V_d = dscr("Vs", [S_LOC, D], BF16)
    QT_d = dscr("QT", [8, 128, NQ], BF16)
    MB_d = dscr("MB", [8, 32, NQ], BF16)
    YA_d = dscr("YA", [8, 128, NQ], BF16)
    YB_d = dscr("YB", [8, 128, NQ], BF16)
    XM_d = dscr("XM", [NQ, D], F32)
    B_KT, B_V, B_QT, B_MB, B_YA, B_YB, B_XM = (Buf(n) for n in "KT V QT MB YA YB XM".split())

    banks = []
    for i in range(8):
        t = nc.alloc_psum_tensor("bank%d" % i, [128, 512], F32)
        banks.append(Tl(t, "bank%d" % i, excl=True))

    def fsz(ap):
        n = 1
        for d in ap.shape[1:]:
            n *= int(d)
        return n

    def mm(out, lhsT, rhs, start, stop, R, W):
        P.add(PE, lambda e: e.matmul(out, lhsT=lhsT, rhs=rhs, start=start, stop=stop), R, W,
              cost=25.0 + 0.52 * max(fsz(out), 64))

    def tr(out, in_, ident, R, W):
        P.add(PE, lambda e: e.transpose(out, in_, ident), R, W, cost=25.0 + 0.52 * max(fsz(out), 64))

    def act(out, in_, func, R, W, bias=None, scale=1.0, accum=None):
        kw = {}
        if bias is not None:
            kw["bias"] = bias
        if accum is not None:
            kw["accum_out"] = accum
        P.add(ACT, lambda e: e.activation(out=out, in_=in_, func=func, scale=scale, **kw), R, W,
              cost=(224.0 + fsz(out)) / 1.2)

    def vcost(eng, out):
        n = fsz(out)
        if eng == POOL:
            return 300.0 + 2.3 * n
        if eng == ACT:
            return (224.0 + n) / 1.2
        return (130.0 + n) / 0.96

    def tt(eng, out, in0, in1, op, R, W):
        P.add(eng, lambda e: e.tensor_tensor(out=out, in0=in0, in1=in1, op=op), R, W, cost=vcost(eng, out))

    def ts(eng, out, in0, s1, s2, op0, op1, R, W):
        if s2 is None:
            P.add(eng, lambda e: e.tensor_scalar(out=out, in0=in0, scalar1=s1, scalar2=None, op0=op0), R, W,
                  cost=vcost(eng, out))
        else:
            P.add(eng, lambda e: e.tensor_scalar(out=out, in0=in0, scalar1=s1, scalar2=s2, op0=op0, op1=op1), R, W,
                  cost=vcost(eng, out))

    def stt(eng, out, in0, scalar, in1, op0, op1, R, W):
        P.add(eng, lambda e: e.scalar_tensor_tensor(out=out, in0=in0, scalar=scalar, in1=in1, op0=op0, op1=op1), R, W,
              cost=vcost(eng, out))

    def cp(eng, out, in_, R, W):
        if eng == ACT:
            P.add(ACT, lambda e: e.copy(out=out, in_=in_), R, W, cost=vcost(eng, out))
        else:
            P.add(eng, lambda e: e.tensor_copy(out=out, in_=in_), R, W, cost=vcost(eng, out))

    def vop(eng, fn, out, R, W):
        P.add(eng, fn, R, W, cost=vcost(eng, out))

    def memset(eng, ap, val, W):
        P.add(eng, lambda e: e.memset(ap, val), [], W, cost=vcost(eng, ap))

    def dma(out, in_, R=(), W=(), q=None):
        nb = 1
        for d in out.shape:
            nb *= int(d)
        nb *= 2 if out.dtype == BF16 else 4
        return P.dma(SP, out, in_, R, W, nbytes=nb)

    main = contextlib.ExitStack()

    def alloc(stack, name, shape, dt):
        t = stack.enter_context(nc.sbuf_tensor(name, list(shape), dt))
        return Tl(t, name)

    ident = alloc(main, "ident", [128, 128], BF16)
    ones = alloc(main, "ones", [128, 128], BF16)
    eps_t = alloc(main, "eps_t", [128, 1], F32)
    flag = alloc(main, "flag_t", [128, 1], F32)
    g1T = alloc(main, "g1T_t", [128, 8], F32)
    g2T = alloc(main, "g2T_t", [128, 8], F32)
    memset(POOL, ident.t[:], 0.0, [ident.b])
    P.add(POOL, lambda e: e.affine_select(out=ident.t[:], in_=ident.t[:], pattern=[[-1, 128]],
                                          compare_op=ALU.not_equal, fill=1.0, base=0, channel_multiplier=1),
          [ident.b], [ident.b])
    memset(POOL, ones.t[:], 1.0, [ones.b])
    memset(POOL, eps_t.t[:], EPS, [eps_t.b])
    dma(flag.t[:], flag_d[:, :], W=[flag.b])
    dma(g1T.t[:], g1T_d[:, :], W=[g1T.b])
    dma(g2T.t[:], g2T_d[:, :], W=[g2T.b])

    cast_rr = [0]

    def load_weight(stack_tiles, dst, dst_cols0, src, src_col0, ncols, nchunk, scaleT=None, cscale=None, blk=512,
                    by_row=False, defer=None, prio_mod=None):
        stg = stack_tiles
        if prio_mod is None:
            prio_mod = ncols
        if dst.cb is None:
            dst.cb = [Buf() for _ in range(nchunk if by_row else dst.t.shape[-1] // 128)]
        if by_row:
            order = [(c, c0) for c in range(nchunk) for c0 in range(0, ncols, blk)]
        else:
            order = [(c, c0) for c0 in range(0, ncols, blk) for c in range(nchunk)]
        for (c, c0) in order:
            prio = ((c0 % prio_mod) / float(prio_mod)) if not by_row else 2.0 + c / float(nchunk)
            item = (lambda c=c, c0=c0: _load_one(stg, dst, dst_cols0, src, src_col0, ncols, c, c0, scaleT, cscale, blk, by_row))
            if defer is None:
                item()
            else:
                defer.append((prio, len(defer), item))

    def run_deferred(defer):
        for _, _, item in sorted(defer, key=lambda t: (t[0], t[1])):
            item()

    def _load_one(stg, dst, dst_cols0, src, src_col0, ncols, c, c0, scaleT, cscale, blk, by_row):
        if True:
            w = min(blk, ncols - c0)
            s = stg[cast_rr[0] % len(stg)]
            dma(s.t[:, 0:w], src[c * 128:(c + 1) * 128, src_col0 + c0: src_col0 + c0 + w], W=[s.b])
            o = dst.t[:, c, dst_cols0 + c0: dst_cols0 + c0 + w]
            if by_row:
                Wb = [dst.cb[c]]
            else:
                Wb = [dst.cb[k] for k in range((dst_cols0 + c0) // 128, (dst_cols0 + c0 + w + 127) // 128)]
            eng = (DVE, ACT)[cast_rr[0] % 2]
            if scaleT is not None:
                sc = scaleT.t[:, c:c + 1]
                R = [s.b, scaleT.b]
                if eng == ACT:
                    act(o, s.t[:, 0:w], AF.Copy, R, Wb, scale=sc)
                else:
                    ts(DVE, o, s.t[:, 0:w], sc, None, ALU.mult, None, R, Wb)
            elif cscale is not None:
                if eng == ACT:
                    act(o, s.t[:, 0:w], AF.Copy, [s.b], Wb, scale=float(cscale))
                else:
                    ts(DVE, o, s.t[:, 0:w], float(cscale), None, ALU.mult, None, [s.b], Wb)
            else:
                cp(eng, o, s.t[:, 0:w], [s.b], Wb)
            cast_rr[0] += 1

    def wb(wt, col0, ncols=128):
        return [wt.cb[k] for k in range(col0 // 128, (col0 + ncols) // 128)]

    def front_end(st, src_rows_ap_fn, nsub, xt, junk, ssq, lnv, rstd, xn, hT, tpbank, R_src=()):
        for s in range(nsub):
            dma(xt[s].t[:], src_rows_ap_fn(s), R=list(R_src), W=[xt[s].b], q=(SP if s % 2 == 0 else POOL))
        for s in range(nsub):
            act(junk.t[:], xt[s].t[:], AF.Square, [xt[s].b], [junk.b, ssq.b], accum=ssq.t[:, s:s + 1])
        act(lnv.t[:, 0:nsub], ssq.t[:, 0:nsub], AF.Ln, [ssq.b, eps_t.b], [lnv.b], bias=eps_t.t[:], scale=1.0 / D)
        act(rstd.t[:, 0:nsub], lnv.t[:, 0:nsub], AF.Exp, [lnv.b], [rstd.b], scale=-0.5)
        for s in range(nsub):
            ts(DVE, xn[s].t[:], xt[s].t[:], rstd.t[:, s:s + 1], None, ALU.mult, None, [xt[s].b, rstd.b], [xn[s].b])
        for c in range(8):
            bk = tpbank[c % len(tpbank)]
            pv = bk.t[:].bitcast(BF16)
            for s in range(nsub):
                tr(pv[:, 128 * s:128 * (s + 1)], xn[s].t[:, 128 * c:128 * (c + 1)], ident.t[:],
                   [xn[s].b, ident.b], [bk.b])
            cp(ACT if c % 2 == 0 else DVE, hT[c].t[:, 0:128 * nsub], pv[:, 0:128 * nsub], [bk.b], [hT[c].b])

    def front_tiles(st, pfx, nsubmax):
        xt = [alloc(st, pfx + "xt%d" % s, [128, D], F32) for s in range(nsubmax)]
        junk = alloc(st, pfx + "junk", [128, D], BF16)
        ssq = alloc(st, pfx + "ssq", [128, 4], F32)
        lnv = alloc(st, pfx + "lnv", [128, 4], F32)
        rstd = alloc(st, pfx + "rstd", [128, 4], F32)
        xn = [alloc(st, pfx + "xn%d" % s, [128, D], BF16) for s in range(nsubmax)]
        return xt, junk, ssq, lnv, rstd, xn

    with contextlib.ExitStack() as st:
        stg = [alloc(st, "a1stg%d" % i, [128, 512], F32) for i in range(12)]
        Wa = alloc(st, "a1W", [128, 8, 2048], BF16)
        wrb = alloc(st, "a1wr", [128, 8, 128], BF16)
        wib = alloc(st, "a1wi", [128, 8, 128], BF16)
        load_weight(stg, Wa, 0, win_d, 0, 2048, 8, scaleT=g1T)
        for (dst, src) in ((wrb, wr_d), (wib, wi_d)):
            for n in range(8):
                s = stg[cast_rr[0] % len(stg)]
                dma(s.t[:, 0:128], src[n, :, :], W=[s.b])
                cp(DVE, dst.t[:, n, :], s.t[:, 0:128], [s.b], [dst.b])
                cast_rr[0] += 1
        cw = alloc(st, "a1cw", [128, 32], F32)
        cb = alloc(st, "a1cb", [128, 8], F32)
        nbr = alloc(st, "a1nbr", [128, 8], F32)
        nbi = alloc(st, "a1nbi", [128, 8], F32)
        lam = alloc(st, "a1lam", [128, 8], F32)
        c1 = alloc(st, "a1c1", [128, 8], F32)
        c2 = alloc(st, "a1c2", [128, 8], F32)
        dma(cw.t[:], cw_d[:, :], W=[cw.b])
        dma(cb.t[:], cb_d[:, :], W=[cb.b])
        dma(nbr.t[:], br_d[:, :], W=[nbr.b])
        dma(nbi.t[:], bi_d[:, :], W=[nbi.b])
        dma(lam.t[:], lam_d[:, :], W=[lam.b])
        ts(DVE, nbr.t[:], nbr.t[:], -1.0, None, ALU.mult, None, [nbr.b], [nbr.b])
        ts(DVE, nbi.t[:], nbi.t[:], -1.0, None, ALU.mult, None, [nbi.b], [nbi.b])
        act(lam.t[:], lam.t[:], AF.Exp, [lam.b], [lam.b], scale=-1.0)
        act(lam.t[:], lam.t[:], AF.Ln, [lam.b], [lam.b], bias=1.0, scale=1.0)
        ts(DVE, c1.t[:], lam.t[:], -8.0, None, ALU.mult, None, [lam.b], [c1.b])
        ts(DVE, c2.t[:], lam.t[:], -16.0, None, ALU.mult, None, [lam.b], [c2.b])

        xt, junk, ssq, lnv, rstd, xn = front_tiles(st, "a1", 4)
        hT = [[alloc(st, "a1hT%d_%d" % (pb, c), [128, 512], BF16) for c in range(8)] for pb in range(2)]
        xr = [alloc(st, "a1xr%d" % e, [128, 515], F32) for e in range(8)]
        xa = [alloc(st, "a1xa%d" % e, [128, 512], F32) for e in range(3)]
        xab = [alloc(st, "a1xab%d" % e, [128, 512], BF16) for e in range(3)]
        tmp = {n: [alloc(st, "a1%s%d" % (n, i), [128, 512], F32) for i in range(3)]
               for n in ("er", "ei", "a", "a2", "mu", "bx", "hs", "gq", "gu", "ge", "gg")}
        yab = [alloc(st, "a1yab%d" % i, [128, 512], BF16) for i in range(3)]
        hst = alloc(st, "a1hst", [128, 8], F32)
        memset(POOL, hst.t[:], 0.0, [hst.b])
        for e in range(8):
            memset(POOL, xr[e].t[:, 0:3], 0.0, [xr[e].b])
        cnt = 0
        for i in range(16):
            T0 = 512 * i
            full = i >= 7
            pb = i % 2
            front_end(st, lambda s: x_d[T0 + 128 * s: T0 + 128 * (s + 1), :], 4, xt, junk, ssq, lnv, rstd, xn,
                      hT[pb], [banks[0], banks[1]])
            if i == 8:
                ts(DVE, hst.t[:], hst.t[:], flag.t[:, 0:1], None, ALU.mult, None, [hst.b, flag.b], [hst.b])
            for e in range(8):
                j = cnt % 3
                cnt += 1
                bx_, br_, bi_, bg_ = banks[2 + (e % 2)], banks[4], banks[5], banks[6 + (e % 2)]
                for c in range(8):
                    mm(bx_.t[:], Wa.t[:, c, 128 * e:128 * (e + 1)], hT[pb][c].t[:], c == 0, c == 7,
                       wb(Wa, 128 * e) + [hT[pb][c].b], [bx_.b])
                cp(ACT, xr[e].t[:, 3:515], bx_.t[:], [bx_.b], [xr[e].b])
                A = xa[j]
                act(A.t[:], bx_.t[:], AF.Identity, [bx_.b, cw.b, cb.b], [A.b], bias=cb.t[:, e:e + 1],
                    scale=cw.t[:, 4 * e + 3:4 * e + 4])
                for k in range(3):
                    stt(DVE, A.t[:], xr[e].t[:, k:k + 512], cw.t[:, 4 * e + k:4 * e + k + 1], A.t[:], ALU.mult, ALU.add,
                        [xr[e].b, cw.b, A.b], [A.b])
                cp(DVE, xab[j].t[:], A.t[:], [A.b], [xab[j].b])
                cp(POOL, xr[e].t[:, 0:3], xr[e].t[:, 512:515], [xr[e].b], [xr[e].b])
                mm(br_.t[:], wrb.t[:, e, :], xab[j].t[:], True, True, [wrb.b, xab[j].b], [br_.b])
                mm(bi_.t[:], wib.t[:, e, :], xab[j].t[:], True, True, [wib.b, xab[j].b], [bi_.b])
                er, ei, a_, a2, mu, bx, hs = (tmp[n][j] for n in ("er", "ei", "a", "a2", "mu", "bx", "hs"))
                act(er.t[:], br_.t[:], AF.Exp, [br_.b, nbr.b], [er.b], bias=nbr.t[:, e:e + 1], scale=-1.0)
                act(ei.t[:], bi_.t[:], AF.Exp, [bi_.b, nbi.b], [ei.b], bias=nbi.t[:, e:e + 1], scale=-1.0)
                act(er.t[:], er.t[:], AF.Ln, [er.b], [er.b], bias=1.0, scale=1.0)
                act(ei.t[:], ei.t[:], AF.Ln, [ei.b], [ei.b], bias=1.0, scale=1.0)
                act(er.t[:], er.t[:], AF.Exp, [er.b], [er.b], scale=-1.0)
                act(ei.t[:], ei.t[:], AF.Exp, [ei.b], [ei.b], scale=-1.0)
                act(a_.t[:], er.t[:], AF.Exp, [er.b, c1.b], [a_.b], scale=c1.t[:, e:e + 1])
                stt(DVE, a2.t[:], a_.t[:], 0.99999994, a_.t[:], ALU.min, ALU.mult, [a_.b], [a2.b])
                act(mu.t[:], a2.t[:], AF.Ln, [a2.b], [mu.b], bias=1.0, scale=-1.0)
                act(mu.t[:], mu.t[:], AF.Exp, [mu.b], [mu.b], scale=0.5)
                tt(DVE, bx.t[:], ei.t[:], A.t[:], ALU.mult, [ei.b, A.b], [bx.b])
                tt(DVE, bx.t[:], bx.t[:], mu.t[:], ALU.mult, [bx.b, mu.b], [bx.b])
                P.add(DVE, lambda e_, o=hs.t[:], d0=a_.t[:], d1=bx.t[:], ini=hst.t[:, e:e + 1]:
                      e_.tensor_tensor_scan(out=o, data0=d0, data1=d1, initial=ini, op0=ALU.mult, op1=ALU.add),
                      [a_.b, bx.b, hst.b], [hs.b], cost=700.0)
                cp(DVE, hst.t[:, e:e + 1], hs.t[:, 511:512], [hs.b], [hst.b])
                if full:
                    for c in range(8):
                        mm(bg_.t[:], Wa.t[:, c, 1024 + 128 * e:1024 + 128 * (e + 1)], hT[pb][c].t[:], c == 0, c == 7,
                           wb(Wa, 1024 + 128 * e) + [hT[pb][c].b], [bg_.b])
                    gq, gu, ge, gg = (tmp[n][j] for n in ("gq", "gu", "ge", "gg"))
                    ts(DVE, gg.t[:], bg_.t[:], -7.0, None, ALU.max, None, [bg_.b], [gg.b])
                    act(gq.t[:], gg.t[:], AF.Square, [gg.b], [gq.b])
                    ts(DVE, gu.t[:], gq.t[:], 0.044715, 1.0, ALU.mult, ALU.add, [gq.b], [gu.b])
                    tt(DVE, gu.t[:], gu.t[:], gg.t[:], ALU.mult, [gu.b, gg.b], [gu.b])
                    act(ge.t[:], gu.t[:], AF.Exp, [gu.b], [ge.b], scale=-1.5957691216)
                    act(ge.t[:], ge.t[:], AF.Ln, [ge.b], [ge.b], bias=1.0, scale=1.0)
                    act(ge.t[:], ge.t[:], AF.Exp, [ge.b], [ge.b], scale=-1.0)
                    tt(DVE, gg.t[:], ge.t[:], bg_.t[:], ALU.mult, [ge.b, bg_.b], [gg.b])
                    yb_ = yab[(i * 8 + e) % 3]
                    tt(DVE, yb_.t[:], gg.t[:], hs.t[:], ALU.mult, [gg.b, hs.b], [yb_.b])
                    if i == 7:
                        dma(YA_d[e, :, 0:128], yb_.t[:, 384:512], R=[yb_.b], W=[B_YA])
                    else:
                        q0 = 128 + 512 * (i - 8)
                        dma(YA_d[e, :, q0:q0 + 512], yb_.t[:], R=[yb_.b], W=[B_YA])
    P.barrier()

    with contextlib.ExitStack() as st:
        stg = [alloc(st, "a2stg%d" % i, [128, 512], F32) for i in range(12)]
        Wq = alloc(st, "a2W", [128, 8, 3072], BF16)
        load_weight(stg, Wq, 0, win_d, 2048, 3072, 8, scaleT=g1T)
        qgb = alloc(st, "a2qg", [128, 128], F32)
        kgb = alloc(st, "a2kg", [128, 128], F32)
        dma(qgb.t[:], qg_d[0:1, :].broadcast_to([128, 128]), W=[qgb.b])
        dma(kgb.t[:], kg_d[0:1, :].broadcast_to([128, 128]), W=[kgb.b])
        tt(DVE, qgb.t[:], qgb.t[:], kgb.t[:], ALU.mult, [qgb.b, kgb.b], [qgb.b])
        blkb = alloc(st, "a2blkb", [128, 32], F32)
        biaso = alloc(st, "a2biaso", [128, 17, 32], F32)
        valido = alloc(st, "a2valido", [128, 17, 32], F32)
        om1b = alloc(st, "a2om1b", [128, 17, 32], F32)
        dma(blkb.t[:], blkbias_d[:, :], W=[blkb.b])
        dma(biaso.t[:], pastb_d[:, :].rearrange("p (o b) -> p o b", o=17), W=[biaso.b])
        dma(om1b.t[:], om1big_d[:, :].rearrange("p (o b) -> p o b", o=17), W=[om1b.b])
        tt(DVE, biaso.t[:], biaso.t[:], blkb.t[:].rearrange("p (o b) -> p o b", o=1).broadcast_to([128, 17, 32]),
           ALU.add, [biaso.b, blkb.b], [biaso.b])
        ts(DVE, valido.t[:], biaso.t[:], -1e29, None, ALU.is_gt, None, [biaso.b], [valido.b])
        kmT = alloc(st, "a2kmT", [128, 8, 32], F32)
        kmTb = alloc(st, "a2kmTb", [128, 8, 32], BF16)
        memset(POOL, kmT.t[:], 0.0, [kmT.b])
        memset(POOL, kmTb.t[:], 0.0, [kmTb.b])

        xt, junk, ssq, lnv, rstd, xn = front_tiles(st, "a2", 4)
        hT = [[alloc(st, "a2hT%d_%d" % (pb, c), [128, 512], BF16) for c in range(8)] for pb in range(2)]
        raw2 = [[alloc(st, "a2raw%d_%d" % (u, s), [128, D], F32) for s in range(4)] for u in range(2)]
        sq = [alloc(st, "a2sq%d" % s, [128, 512], F32) for s in range(2)]
        ssk2 = [alloc(st, "a2ssk%d" % u, [128, 32], F32) for u in range(2)]
        lnk2 = [alloc(st, "a2lnk%d" % u, [128, 32], F32) for u in range(2)]
        rsk2 = [alloc(st, "a2rsk%d" % u, [128, 32], F32) for u in range(2)]
        nrm2 = [[alloc(st, "a2nrm%d_%d" % (u, s), [128, D], BF16) for s in range(4)] for u in range(2)]
        kTs = [alloc(st, "a2kTs%d" % s, [128, 512], BF16) for s in range(3)]
        qst = [[alloc(st, "a2qst%d_%d" % (u, h), [128, 512], BF16) for h in range(8)] for u in range(2)]
        vb = [alloc(st, "a2vb%d" % s, [128, D], BF16) for s in range(2)]
        gb = alloc(st, "a2gb", [128, 8, 32], F32)
        top8 = alloc(st, "a2top8", [128, 8, 8], F32)
        sel = alloc(st, "a2sel", [128, 8, 32], F32)
        mbq = alloc(st, "a2mbq", [128, 8, 32], BF16)
        mbT = [alloc(st, "a2mbT%d" % s, [32, 8, 128], BF16) for s in range(2)]
        rr = [0]

        def qk_unit(pb, col0, is_q, i):
            u = 1 if is_q else 0
            raw, nrm, ssk, lnk, rsk = raw2[u], nrm2[u], ssk2[u], lnk2[u], rsk2[u]
            for s in range(4):
                for half in range(2):
                    bk = banks[2 + (rr[0] % 4)]
                    rr[0] += 1
                    for c in range(8):
                        mm(bk.t[:], hT[pb][c].t[:, 128 * s:128 * (s + 1)],
                           Wq.t[:, c, col0 + 512 * half: col0 + 512 * (half + 1)], c == 0, c == 7,
                           [hT[pb][c].b] + wb(Wq, col0 + 512 * half, 512), [bk.b])
                    cp(ACT, raw[s].t[:, 512 * half:512 * (half + 1)], bk.t[:], [bk.b], [raw[s].b])
                    sqt = sq[(2 * s + half) % 2]
                    act(sqt.t[:], bk.t[:], AF.Square, [bk.b], [sqt.b])
                    o0 = 8 * s + 4 * half
                    P.add(DVE, lambda e_, o=ssk.t[:, o0:o0 + 4], in_=sqt.t[:].rearrange("p (h d) -> p h d", h=4):
                          e_.tensor_reduce(out=o, in_=in_, axis=AX.X, op=ALU.add), [sqt.b], [ssk.b])
            act(lnk.t[:], ssk.t[:], AF.Ln, [ssk.b, eps_t.b], [lnk.b], bias=eps_t.t[:], scale=1.0 / 128)
            act(rsk.t[:], lnk.t[:], AF.Exp, [lnk.b], [rsk.b], scale=-0.5)
            for s in range(4):
                rv = rsk.t[:, 8 * s:8 * s + 8].rearrange("p (h o) -> p h o", o=1).broadcast_to([128, 8, 128])
                n3 = nrm[s].t[:].rearrange("p (h d) -> p h d", h=8)
                r3 = raw[s].t[:].rearrange("p (h d) -> p h d", h=8)
                if is_q:
                    tt(DVE, r3, r3, rv, ALU.mult, [raw[s].b, rsk.b], [raw[s].b])
                    gv = qgb.t[:].rearrange("p (o d) -> p o d", o=1).broadcast_to([128, 8, 128])
                    tt(DVE, n3, r3, gv, ALU.mult, [raw[s].b, qgb.b], [nrm[s].b])
                else:
                    tt(DVE, n3, r3, rv, ALU.mult, [raw[s].b, rsk.b], [nrm[s].b])

        for i in range(16):
            T0 = 512 * i
            full = i >= 7
            pb = i % 2
            front_end(st, lambda s: x_d[T0 + 128 * s: T0 + 128 * (s + 1), :], 4, xt, junk, ssq, lnv, rstd, xn,
                      hT[pb], [banks[0], banks[1]])
            for s in range(4):
                v_ = vb[s % 2]
                for half in range(2):
                    bk = banks[2 + (rr[0] % 4)]
                    rr[0] += 1
                    for c in range(8):
                        mm(bk.t[:], hT[pb][c].t[:, 128 * s:128 * (s + 1)],
                           Wq.t[:, c, 2048 + 512 * half: 2048 + 512 * (half + 1)], c == 0, c == 7,
                           [hT[pb][c].b] + wb(Wq, 2048 + 512 * half, 512), [bk.b])
                    cp(ACT if half == 0 else DVE, v_.t[:, 512 * half:512 * (half + 1)], bk.t[:], [bk.b], [v_.b])
                dma(V_d[T0 + 128 * s:T0 + 128 * (s + 1), :], v_.t[:], R=[v_.b], W=[B_V], q=POOL)
            qk_unit(pb, 1024, False, i)
            for h in range(8):
                bk = banks[6 + (h % 2)]
                pv = bk.t[:].bitcast(BF16)
                for s in range(4):
                    tr(pv[:, 128 * s:128 * (s + 1)], nrm2[0][s].t[:, 128 * h:128 * (h + 1)], ident.t[:],
                       [nrm2[0][s].b, ident.b], [bk.b])
                kt_ = kTs[h % 3]
                cp(ACT, kt_.t[:], pv[:, 0:512], [bk.b], [kt_.b])
                dma(KT_d[h, :, T0:T0 + 512], kt_.t[:], R=[kt_.b], W=[B_KT])
                P.add(DVE, lambda e_, o=kmT.t[:, h, 2 * i:2 * i + 2], in_=kt_.t[:].rearrange("p (b k) -> p b k", b=2):
                      e_.tensor_reduce(out=o, in_=in_, axis=AX.X, op=ALU.add), [kt_.b], [kmT.b])
            ts(DVE, kmTb.t[:, :, 2 * i:2 * i + 2], kmT.t[:, :, 2 * i:2 * i + 2], 1.0 / 256, None, ALU.mult, None,
               [kmT.b], [kmTb.b])
            if not full:
                continue
            qk_unit(pb, 0, True, i)
            qts = []
            for h in range(8):
                bk = banks[6 + (h % 2)]
                pv = bk.t[:].bitcast(BF16)
                for s in range(4):
                    tr(pv[:, 128 * s:128 * (s + 1)], nrm2[1][s].t[:, 128 * h:128 * (h + 1)], ident.t[:],
                       [nrm2[1][s].b, ident.b], [bk.b])
                qs_ = qst[pb][h]
                qt_ap = qs_.t[:]
                cp(ACT if h % 2 == 0 else DVE, qt_ap, pv[:, 0:512], [bk.b], [qs_.b])
                qts.append((qt_ap, qs_.b))
                if i == 7:
                    dma(QT_d[h, :, 0:128], qt_ap[:, 384:512], R=[qs_.b], W=[B_QT])
                else:
                    q0 = 128 + 512 * (i - 8)
                    dma(QT_d[h, :, q0:q0 + 512], qt_ap, R=[qs_.b], W=[B_QT])
            for s in (range(3, 4) if i == 7 else range(4)):
                o = 2 * i + s // 2
                oi = o - 15
                bk = banks[2 + (rr[0] % 4)]
                rr[0] += 1
                g3 = bk.t[:, 0:256].rearrange("p (h b) -> p h b", h=8)
                for h in range(8):
                    mm(bk.t[:, 32 * h:32 * (h + 1)], qts[h][0][:, 128 * s:128 * (s + 1)], kmTb.t[:, h, :], True, True,
                       [qts[h][1], kmTb.b], [bk.b])
                bo = biaso.t[:, oi:oi + 1, :].broadcast_to([128, 8, 32])
                tt(DVE, gb.t[:], g3, bo, ALU.add, [bk.b, biaso.b], [gb.b])
                for h in range(8):
                    P.add(DVE, lambda e_, o_=top8.t[:, h, :], in_=gb.t[:, h, :]: e_.max(out=o_, in_=in_), [gb.b], [top8.b])
                tt(DVE, sel.t[:], gb.t[:], top8.t[:, :, 2:3].broadcast_to([128, 8, 32]), ALU.is_ge, [gb.b, top8.b], [sel.b])
                tt(DVE, sel.t[:], sel.t[:], valido.t[:, oi:oi + 1, :].broadcast_to([128, 8, 32]), ALU.mult,
                   [sel.b, valido.b], [sel.b])
                stt(DVE, mbq.t[:], sel.t[:], BIG, om1b.t[:, oi:oi + 1, :].broadcast_to([128, 8, 32]), ALU.mult, ALU.add,
                    [sel.b, om1b.b], [mbq.b])
                bk2 = banks[6 + (rr[0] % 2)]
                pv2 = bk2.t[:].bitcast(BF16)
                for h in range(8):
                    tr(pv2[0:32, 128 * h:128 * (h + 1)], mbq.t[:, h, :], ident.t[:], [mbq.b, ident.b], [bk2.b])
                m_ = mbT[rr[0] % 2]
                cp(DVE, m_.t[:].rearrange("p h q -> p (h q)"), pv2[0:32, 0:1024], [bk2.b], [m_.b])
                qidx = (T0 + 128 * s) - OWN0
                dma(MB_d[:, :, qidx:qidx + 128].rearrange("h b q -> b h q"), m_.t[:], R=[m_.b], W=[B_MB], q=POOL)
    P.barrier()

    with contextlib.ExitStack() as st:
        esel = alloc(st, "besel", [128, 32, 128], BF16)
        cm = alloc(st, "bcm", [128, 4, 512], BF16)
        stgb = alloc(st, "bstg", [128, 4096], F32)
        memset(DVE, esel.t[:], 0.0, [esel.b])
        dma(stgb.t[0:32, :], esel_d[:, :], W=[stgb.b])
        cp(DVE, esel.t[0:32, :, :].rearrange("p b m -> p (b m)"), stgb.t[0:32, :], [stgb.b, esel.b], [esel.b])
        dma(stgb.t[:, 0:2048], cmask_d[:, :], R=[], W=[stgb.b])
        cp(DVE, cm.t[:].rearrange("p r q -> p (r q)"), stgb.t[:, 0:2048], [stgb.b], [cm.b])
        KTh = [alloc(st, "bKT%d" % i, [128, S_LOC], BF16) for i in range(2)]
        Vh = [alloc(st, "bV%d" % i, [128, 64, 128], BF16) for i in range(2)]
        QTh = [alloc(st, "bQT%d" % i, [128, NQ], BF16) for i in range(2)]
        MBh = [alloc(st, "bMB%d" % i, [128, NQ], BF16) for i in range(2)]
        for i in range(2):
            memset(DVE, MBh[i].t[:], 0.0, [MBh[i].b])
        pt = [alloc(st, "bpt%d" % i, [128, 512], BF16) for i in range(6)]
        dr = [alloc(st, "bdr%d" % i, [128, 512], F32) for i in range(2)]
        yo = [alloc(st, "byo%d" % i, [128, 512], BF16) for i in range(2)]
        dac = [[alloc(st, "bdac%d_%d" % (i, k), [128, 512], F32) for k in range(4)] for i in range(2)]
        dbf = [alloc(st, "bdbf%d" % i, [128, 512], BF16) for i in range(2)]
        cntS = 0
        cntJ = 0
        for h in range(8):
            hb = h % 2
            for q4 in range(4):
                dma(KTh[hb].t[:, 2048 * q4:2048 * (q4 + 1)], KT_d[h, :, 2048 * q4:2048 * (q4 + 1)], R=[B_KT], W=[KTh[hb].b],
                    q=(SP if q4 % 2 == 0 else POOL))
                dma(Vh[hb].t[:, 16 * q4:16 * (q4 + 1), :],
                    V_d[2048 * q4:2048 * (q4 + 1), 128 * h:128 * (h + 1)].rearrange("(n p) d -> p n d", p=128),
                    R=[B_V], W=[Vh[hb].b], q=(POOL if q4 % 2 == 0 else SP))
            dma(QTh[hb].t[:], QT_d[h, :, :], R=[B_QT], W=[QTh[hb].b])
            dma(MBh[hb].t[0:32, :], MB_d[h, :, :], R=[B_MB, MBh[hb].b], W=[MBh[hb].b], q=POOL)
            for j in range(9):
                if j == 0:
                    q0, N, nk, kd0 = 0, 128, 32, 31
                else:
                    q0, N, nk, kd0 = 128 + 512 * (j - 1), 512, 32 + 4 * j, 32 + 4 * (j - 1)
                bo_, bd_ = banks[3 + 2 * (cntJ % 2)], banks[4 + 2 * (cntJ % 2)]
                cntJ += 1
                qv = QTh[hb].t[:, q0:q0 + N]
                mv = MBh[hb].t[:, q0:q0 + N]

                def s_stage(kt):
                    bs = banks[(0, 1, 2, 7)[cntS % 4]]
                    diag = kt >= kd0
                    mm(bs.t[:, 0:N], KTh[hb].t[:, 128 * kt:128 * (kt + 1)], qv, True, False,
                       [KTh[hb].b, QTh[hb].b], [bs.b])
                    mm(bs.t[:, 0:N], esel.t[:, kt // 2, :], mv, False, not diag, [esel.b, MBh[hb].b], [bs.b])
                    if diag:
                        mm(bs.t[:, 0:N], ident.t[:], cm.t[:, kt - kd0, 0:N], False, True, [ident.b, cm.b], [bs.b])
                    p_ = pt[cntS % 6]
                    act(p_.t[:, 0:N], bs.t[:, 0:N], AF.Exp, [bs.b], [p_.b], scale=SCALE)
                    return p_

                daccs = dac[cntJ % 2]

                def o_stage(kt, p_):
                    mm(bo_.t[:, 0:N], Vh[hb].t[:, kt, :], p_.t[:, 0:N], kt == 0, kt == nk - 1, [Vh[hb].b, p_.b], [bo_.b])
                    dacc = daccs[kt % 4]
                    if kt < 4:
                        cp(DVE, dacc.t[:, 0:N], p_.t[:, 0:N], [p_.b], [dacc.b])
                    else:
                        tt(DVE, dacc.t[:, 0:N], dacc.t[:, 0:N], p_.t[:, 0:N], ALU.add, [dacc.b, p_.b], [dacc.b])

                pend = []
                for kt in range(nk):
                    p_ = s_stage(kt)
                    cntS += 1
                    pend.append((kt, p_))
                    if len(pend) > 1:
                        o_stage(*pend.pop(0))
                while pend:
                    o_stage(*pend.pop(0))
                d_ = dr[cntJ % 2]
                y_ = yo[cntJ % 2]
                db_ = dbf[cntJ % 2]
                tt(DVE, daccs[0].t[:, 0:N], daccs[0].t[:, 0:N], daccs[1].t[:, 0:N], ALU.add, [daccs[0].b, daccs[1].b], [daccs[0].b])
                tt(DVE, daccs[2].t[:, 0:N], daccs[2].t[:, 0:N], daccs[3].t[:, 0:N], ALU.add, [daccs[2].b, daccs[3].b], [daccs[2].b])
                tt(DVE, db_.t[:, 0:N], daccs[0].t[:, 0:N], daccs[2].t[:, 0:N], ALU.add, [daccs[0].b, daccs[2].b], [db_.b])
                mm(bd_.t[:, 0:N], ones.t[:], db_.t[:, 0:N], True, True, [ones.b, db_.b], [bd_.b])
                act(d_.t[:, 0:N], bd_.t[:, 0:N], AF.Ln, [bd_.b], [d_.b])
                act(d_.t[:, 0:N], d_.t[:, 0:N], AF.Exp, [d_.b], [d_.b], scale=-1.0)
                tt(DVE, y_.t[:, 0:N], bo_.t[:, 0:N], d_.t[:, 0:N], ALU.mult, [bo_.b, d_.b], [y_.b])
                dma(YB_d[h, :, q0:q0 + N], y_.t[:, 0:N], R=[y_.b], W=[B_YB])
    P.barrier()

    with contextlib.ExitStack() as st:
        stg = [alloc(st, "c1stg%d" % i, [128, 512], F32) for i in range(12)]
        Wg = alloc(st, "c1Wg", [128, 8, 2048], BF16)
        PAw = alloc(st, "c1PA", [128, 8, D], BF16)
        PBw = alloc(st, "c1PB", [128, 8, D], BF16)
        WOw = alloc(st, "c1WO", [128, 8, D], BF16)
        dfr = []
        load_weight(stg, Wg, 0, win_d, 5120, 2048, 8, scaleT=g1T, defer=dfr, prio_mod=1024)
        load_weight(stg, PAw, 0, pa_d, 0, D, 8, defer=dfr)
        load_weight(stg, PBw, 0, pb_d, 0, D, 8, defer=dfr)
        run_deferred(dfr)
        load_weight(stg, WOw, 0, wo_d, 0, D, 8)
        xt, junk, ssq, lnv, rstd, xn = front_tiles(st, "c1", 4)
        hT = [[alloc(st, "c1hT%d_%d" % (pb, c), [128, 512], BF16) for c in range(8)] for pb in range(2)]
        yaT = [alloc(st, "c1ya%d" % pb, [128, 8, 512], BF16) for pb in range(2)]
        ybT = [alloc(st, "c1yb%d" % pb, [128, 8, 512], BF16) for pb in range(2)]
        mT = [alloc(st, "c1mT%d" % c, [128, 512], BF16) for c in range(8)]
        tA = [alloc(st, "c1tA%d" % i, [128, 512], F32) for i in range(2)]
        tB = [alloc(st, "c1tB%d" % i, [128, 512], F32) for i in range(2)]
        xo = [alloc(st, "c1xo%d" % i, [128, D], F32) for i in range(2)]
        rr = 0
        for j in range(9):
            if j == 0:
                q0, N = 0, 128
            else:
                q0, N = 128 + 512 * (j - 1), 512
            nsub = N // 128
            pb = j % 2
            T0 = OWN0 + q0
            front_end(st, lambda s: x_d[T0 + 128 * s: T0 + 128 * (s + 1), :], nsub, xt, junk, ssq, lnv, rstd, xn,
                      hT[pb], [banks[0], banks[1]])
            dma(yaT[pb].t[:, :, 0:N], YA_d[:, :, q0:q0 + N].rearrange("c p q -> p c q"), R=[B_YA], W=[yaT[pb].b])
            dma(ybT[pb].t[:, :, 0:N], YB_d[:, :, q0:q0 + N].rearrange("c p q -> p c q"), R=[B_YB], W=[ybT[pb].b], q=POOL)
            for e in range(8):
                bgA, bgB, bpA, bpB = banks[2], banks[3], banks[4], banks[5]
                for c in range(8):
                    mm(bgA.t[:, 0:N], Wg.t[:, c, 128 * e:128 * (e + 1)], hT[pb][c].t[:, 0:N], c == 0, c == 7,
                       wb(Wg, 128 * e) + [hT[pb][c].b], [bgA.b])
                for c in range(8):
                    mm(bgB.t[:, 0:N], Wg.t[:, c, 1024 + 128 * e:1024 + 128 * (e + 1)], hT[pb][c].t[:, 0:N], c == 0, c == 7,
                       wb(Wg, 1024 + 128 * e) + [hT[pb][c].b], [bgB.b])
                for c in range(8):
                    mm(bpA.t[:, 0:N], PAw.t[:, c, 128 * e:128 * (e + 1)], yaT[pb].t[:, c, 0:N], c == 0, c == 7,
                       wb(PAw, 128 * e) + [yaT[pb].b], [bpA.b])
                for c in range(8):
                    mm(bpB.t[:, 0:N], PBw.t[:, c, 128 * e:128 * (e + 1)], ybT[pb].t[:, c, 0:N], c == 0, c == 7,
                       wb(PBw, 128 * e) + [ybT[pb].b], [bpB.b])
                a_, b_ = tA[e % 2], tB[e % 2]
                act(a_.t[:, 0:N], bgA.t[:, 0:N], AF.Exp, [bgA.b], [a_.b], scale=-1.0)
                act(b_.t[:, 0:N], bgB.t[:, 0:N], AF.Exp, [bgB.b], [b_.b], scale=-1.0)
                act(a_.t[:, 0:N], a_.t[:, 0:N], AF.Ln, [a_.b], [a_.b], bias=1.0, scale=1.0)
                act(b_.t[:, 0:N], b_.t[:, 0:N], AF.Ln, [b_.b], [b_.b], bias=1.0, scale=1.0)
                act(a_.t[:, 0:N], a_.t[:, 0:N], AF.Exp, [a_.b], [a_.b], scale=-1.0)
                act(b_.t[:, 0:N], b_.t[:, 0:N], AF.Exp, [b_.b], [b_.b], scale=-1.0)
                tt(DVE, a_.t[:, 0:N], a_.t[:, 0:N], bpA.t[:, 0:N], ALU.mult, [a_.b, bpA.b], [a_.b])
                tt(DVE, b_.t[:, 0:N], b_.t[:, 0:N], bpB.t[:, 0:N], ALU.mult, [b_.b, bpB.b], [b_.b])
                tt(DVE, mT[e].t[:, 0:N], a_.t[:, 0:N], b_.t[:, 0:N], ALU.add, [a_.b, b_.b], [mT[e].b])
            for s in range(nsub):
                xo_ = xo[s % 2]
                for half in range(2):
                    bk = banks[6 + (rr % 2)]
                    rr += 1
                    for e in range(8):
                        mm(bk.t[:], mT[e].t[:, 128 * s:128 * (s + 1)], WOw.t[:, e, 512 * half:512 * (half + 1)],
                           e == 0, e == 7, [mT[e].b] + wb(WOw, 512 * half, 512), [bk.b])
                    tt(DVE, xo_.t[:, 512 * half:512 * (half + 1)], bk.t[:], xt[s].t[:, 512 * half:512 * (half + 1)],
                       ALU.add, [bk.b, xt[s].b], [xo_.b])
                dma(XM_d[q0 + 128 * s:q0 + 128 * (s + 1), :], xo_.t[:], R=[xo_.b], W=[B_XM])
    P.barrier()

    with contextlib.ExitStack() as st:
        NT = 384
        xt, junk, ssq, lnv, rstd, xn = front_tiles(st, "c2", 3)
        tmp = {n: [alloc(st, "c2%s%d" % (n, i), [128, NT], F32) for i in range(3)] for n in ("cv", "sq", "u", "cc")}
        stg = [t for n in ("cv", "sq", "u", "cc") for t in tmp[n]]
        WU = alloc(st, "c2WU", [128, 8, DFF], BF16)
        WG = alloc(st, "c2WG", [128, 8, DFF], BF16)
        WD = alloc(st, "c2WD", [128, NF, D], BF16)
        dfr = []
        load_weight(stg, WU, 0, wu_d, 0, DFF, 8, scaleT=g2T, defer=dfr, blk=384)
        load_weight(stg, WG, 0, wg_d, 0, DFF, 8, scaleT=g2T, defer=dfr, blk=384)
        run_deferred(dfr)
        load_weight(stg, WD, 0, wd_d, 0, D, NF, by_row=True, blk=384)
        kc = alloc(st, "c2kc", [128, 2], F32)
        memset(POOL, kc.t[:, 0:1], 22.34, [kc.b])
        memset(POOL, kc.t[:, 1:2], 1.5957691216 * 22.34, [kc.b])
        fcw = alloc(st, "c2fcw", [128, 66], F32)
        fcb = alloc(st, "c2fcb", [128, 22], F32)
        dma(fcw.t[:], fcw_d[:, :], W=[fcw.b])
        dma(fcb.t[:], fcb_d[:, :], W=[fcb.b])
        hT = [[alloc(st, "c2hT%d_%d" % (pb, c), [128, NT], BF16) for c in range(8)] for pb in range(2)]
        upb = [alloc(st, "c2upb%d" % i, [128, NT + 2], F32) for i in range(3)]
        hbk = [(banks[2 + k].t, banks[2 + k].b) for k in range(6)]
        hal = alloc(st, "c2hal", [128, NF, 2], F32)
        memset(POOL, hal.t[:], 0.0, [hal.b])
        actT = [alloc(st, "c2act%d" % f, [128, NT], BF16) for f in range(NF)]
        rr = 0
        tiles = [(0, 128)] + [(128 + NT * k, NT) for k in range(10)] + [(128 + 3840, 256)]
        for j, (q0, N) in enumerate(tiles):
            nsub = N // 128
            pb = j % 2
            front_end(st, lambda s: XM_d[q0 + 128 * s: q0 + 128 * (s + 1), :], nsub, xt, junk, ssq, lnv, rstd, xn,
                      hT[pb], [banks[0], banks[1]], R_src=[B_XM])
            for f in range(NF):
                (bu_ap, bu_b), (bg_ap, bg_b) = hbk[f % 3], hbk[3 + (f % 3)]
                for c in range(8):
                    mm(bu_ap[:, 0:N], WU.t[:, c, 128 * f:128 * (f + 1)], hT[pb][c].t[:, 0:N], c == 0, c == 7,
                       wb(WU, 128 * f) + [hT[pb][c].b], [bu_b])
                for c in range(8):
                    mm(bg_ap[:, 0:N], WG.t[:, c, 128 * f:128 * (f + 1)], hT[pb][c].t[:, 0:N], c == 0, c == 7,
                       wb(WG, 128 * f) + [hT[pb][c].b], [bg_b])
                u_ = upb[f % 3]
                cv, sq_, uu, gts = (tmp[n][f % 3] for n in ("cv", "sq", "u", "cc"))
                ex = sq_
                cp(ACT, u_.t[:, 0:2], hal.t[:, f, :], [hal.b], [u_.b])
                cp(ACT, u_.t[:, 2:2 + N], bu_ap[:, 0:N], [bu_b], [u_.b])
                act(cv.t[:, 0:N], bu_ap[:, 0:N], AF.Identity, [bu_b, fcw.b, fcb.b], [cv.b], bias=fcb.t[:, f:f + 1],
                    scale=fcw.t[:, 3 * f + 2:3 * f + 3])
                if j == 0:
                    ts(POOL, hal.t[:, f, :], u_.t[:, N:N + 2], flag.t[:, 0:1], None, ALU.mult, None, [u_.b, flag.b], [hal.b])
                else:
                    cp(POOL, hal.t[:, f, :], u_.t[:, N:N + 2], [u_.b], [hal.b])
                for k in range(2):
                    stt(DVE, cv.t[:, 0:N], u_.t[:, k:k + N], fcw.t[:, 3 * f + k:3 * f + k + 1], cv.t[:, 0:N], ALU.mult, ALU.add,
                        [u_.b, fcw.b, cv.b], [cv.b])
                act(sq_.t[:, 0:N], cv.t[:, 0:N], AF.Square, [cv.b], [sq_.b])
                stt(DVE, uu.t[:, 0:N], sq_.t[:, 0:N], 0.044715, cv.t[:, 0:N], ALU.mult, ALU.mult, [sq_.b, cv.b], [uu.b])
                stt(DVE, uu.t[:, 0:N], uu.t[:, 0:N], 1.0, cv.t[:, 0:N], ALU.mult, ALU.add, [uu.b, cv.b], [uu.b])
                act(uu.t[:, 0:N], uu.t[:, 0:N], AF.Relu, [uu.b, kc.b], [uu.b], bias=kc.t[:, 0:1], scale=1.0)
                act(ex.t[:, 0:N], uu.t[:, 0:N], AF.Exp, [uu.b, kc.b], [ex.b], bias=kc.t[:, 1:2], scale=-1.5957691216)
                act(ex.t[:, 0:N], ex.t[:, 0:N], AF.Ln, [ex.b], [ex.b], bias=1.0, scale=1.0)
                act(ex.t[:, 0:N], ex.t[:, 0:N], AF.Exp, [ex.b], [ex.b], scale=-1.0)
                tt(DVE, ex.t[:, 0:N], ex.t[:, 0:N], cv.t[:, 0:N], ALU.mult, [ex.b, cv.b], [ex.b])
                tt(DVE, actT[f].t[:, 0:N], ex.t[:, 0:N], bg_ap[:, 0:N], ALU.mult, [ex.b, bg_b], [actT[f].b])
            if j == 0:
                continue
            for s in range(nsub):
                for half in range(2):
                    bk = banks[rr % 2]
                    rr += 1
                    for f in range(NF):
                        mm(bk.t[:], actT[f].t[:, 128 * s:128 * (s + 1)], WD.t[:, f, 512 * half:512 * (half + 1)],
                           f == 0, f == NF - 1, [actT[f].b, WD.cb[f]], [bk.b])
                    tt(DVE, xt[s].t[:, 512 * half:512 * (half + 1)], bk.t[:], xt[s].t[:, 512 * half:512 * (half + 1)],
                       ALU.add, [bk.b, xt[s].b], [xt[s].b])
                r0 = q0 - 128 + 128 * s
                dma(out_d[r0:r0 + 128, :], xt[s].t[:], R=[xt[s].b], W=[])
    P.emit()
    main.close()
    return nc


def _host_consts():
    o = np.arange(15, 32)[:, None]
    b = np.arange(32)[None, :]
    pastb = np.where(b < o, 0.0, -1e30).astype(np.float32)
    om1big = (np.where(b == o, 1.0, 0.0) - 1.0).astype(np.float32) * BIG
    pastb = np.broadcast_to(pastb.reshape(1, -1), (128, 17 * 32)).copy()
    om1big = np.broadcast_to(om1big.reshape(1, -1), (128, 17 * 32)).copy()
    esel = np.zeros((32, 32, 128), np.float32)
    for k in range(32):
        esel[k, k, :] = 1.0
    esel = esel.reshape(32, 32 * 128)
    k = np.arange(128)[:, None, None]
    r = np.arange(4)[None, :, None]
    q = np.arange(512)[None, None, :]
    cmask = np.where(128 * r + k <= q, 0.0, -BIG).astype(np.float32).reshape(128, 4 * 512)
    return pastb, om1big, esel, cmask


def _pc(v, n):
    return np.ascontiguousarray(np.asarray(v, np.float32).reshape(n, 128).T)


_NC_CACHE = {}


def kernel(x, norm1_g, w_in, conv_w, conv_b, w_r, b_r, w_i, b_i, lru_lambda, q_norm_g, k_norm_g,
           w_proj_rnn, w_proj_attn, w_out, norm2_g, w_up, w_gate, ffn_conv_w, ffn_conv_b, w_down):
    x = np.asarray(x, np.float32)
    f = lambda a: np.ascontiguousarray(np.asarray(a, np.float32))
    pastb, om1big, esel, cmask = _host_consts()
    cw = np.asarray(conv_w[0], np.float32)
    cwl = np.ascontiguousarray(cw.T.reshape(8, 128, 4).transpose(1, 0, 2).reshape(128, 32))
    fw = np.asarray(ffn_conv_w[0], np.float32)
    fwl = np.ascontiguousarray(fw.T.reshape(NF, 128, 3).transpose(1, 0, 2).reshape(128, 66))
    common = {
        "pastb": pastb, "om1big": om1big, "esel": esel, "cmask": cmask,
        "g1T": _pc(norm1_g[0], 8), "g2T": _pc(norm2_g[0], 8), "cw": cwl, "cb": _pc(conv_b[0], 8),
        "brT": _pc(np.asarray(b_r[0]).reshape(-1), 8), "biT": _pc(np.asarray(b_i[0]).reshape(-1), 8),
        "lamT": _pc(lru_lambda[0], 8), "qg": f(q_norm_g[0]).reshape(1, 128), "kg": f(k_norm_g[0]).reshape(1, 128),
        "fcw": fwl, "fcb": _pc(ffn_conv_b[0], NF),
        "w_in": f(w_in[0]), "w_r": f(w_r[0]), "w_i": f(w_i[0]), "w_pa": f(w_proj_rnn[0]), "w_pb": f(w_proj_attn[0]),
        "w_out": f(w_out[0]), "w_up": f(w_up[0]), "w_gate": f(w_gate[0]), "w_down": f(w_down[0]),
    }
    in_maps = []
    for core in range(8):
        b, half = core // 2, core % 2
        if half == 0:
            xl = np.zeros((S_LOC, D), np.float32)
            xl[4096:] = x[b, 0:4096]
            flag = np.zeros((128, 1), np.float32)
            blkbias = np.zeros((128, 32), np.float32)
            blkbias[:, 0:16] = -1e30
        else:
            xl = np.ascontiguousarray(x[b])
            flag = np.ones((128, 1), np.float32)
            blkbias = np.zeros((128, 32), np.float32)
        m = dict(common)
        m.update({"xl": xl, "flag": flag, "blkbias": blkbias})
        in_maps.append(m)
    if "nc" not in _NC_CACHE:
        _NC_CACHE["nc"] = build_program()
    res = run_bass_kernel_spmd(_NC_CACHE["nc"], in_maps, core_ids=list(range(8)))
    out = np.empty((4, 8192, D), np.float32)
    for core in range(8):
        b, half = core // 2, core % 2
        out[b, 4096 * half:4096 * (half + 1)] = res.results[core]["out"]
    return out
```

```python
import contextlib
import numpy as np
import concourse.bass as bass
import concourse.mybir as mybir
from concourse.bass_utils import run_bass_kernel_spmd

F32 = mybir.dt.float32
BF16 = mybir.dt.bfloat16
ALU = mybir.AluOpType
AF = mybir.ActivationFunctionType
AX = mybir.AxisListType

PE, ACT, DVE, POOL, SP = "tensor", "scalar", "vector", "gpsimd", "sync"
COMPUTE = (PE, ACT, DVE, POOL)

D = 1024
S_LOC = 8192
OWN0 = 3968
NQ = 4224
DFF = 2816
NF = 22
BIG = 30000.0
EPS = 1e-6
SCALE = 128 ** -0.5


class Buf:
    __slots__ = ("name", "last_w", "readers", "excl")

    def __init__(self, name="", excl=False):
        self.name = name
        self.last_w = None
        self.readers = []
        self.excl = excl


class Op:
    __slots__ = ("eng", "fn", "deps", "users", "signal", "is_dma", "sem", "semval", "cost", "idx",
                 "nleft", "ready", "fin", "start", "seg", "pos", "deps_eff", "weak")

    def __init__(self, eng, fn, is_dma, cost):
        self.eng = eng
        self.fn = fn
        self.deps = []
        self.users = []
        self.signal = False
        self.is_dma = is_dma
        self.sem = None
        self.semval = None
        self.cost = cost
        self.start = 0.0
        self.fin = 0.0
        self.weak = set()


SYNC_LAT = 600.0


class Prog:
    def __init__(self, nc, n_dma_sems=32):
        self.nc = nc
        self.n_dma_sems = n_dma_sems
        self.segments = [[]]

    def add(self, eng, fn, reads=(), writes=(), is_dma=False, cost=500.0):
        op = Op(eng, fn, is_dma, cost)
        ex = [b for b in reads if b.excl and b not in writes]
        if ex:
            writes = list(writes) + ex
        strong = {}
        order = []

        def note(d, is_strong):
            if d is op:
                return
            k = id(d)
            if k not in strong:
                strong[k] = False
                order.append(d)
            if is_strong:
                strong[k] = True

        for b in reads:
            if b.last_w is not None:
                note(b.last_w, True)
        for b in writes:
            if b.last_w is not None:
                note(b.last_w, b.excl and b.last_w.eng != eng)
            for r in b.readers:
                note(r, True)
        for d in order:
            op.deps.append(d)
            if not strong[id(d)] and d.eng == eng and not d.is_dma and not is_dma:
                op.weak.add(id(d))
        for b in reads:
            b.readers.append(op)
        for b in writes:
            b.last_w = op
            b.readers = []
        op.seg = len(self.segments) - 1
        self.segments[-1].append(op)
        return op

    def dma(self, q, out, in_, reads=(), writes=(), nbytes=65536):
        return self.add(SP, lambda e: e.dma_start(out=out, in_=in_), reads, writes, is_dma=True,
                        cost=2000.0 + nbytes / 60.0)

    def barrier(self):
        self.segments.append([])

    def _schedule(self, ops, segi):
        import heapq
        engs = (PE, ACT, DVE, POOL, SP)
        for i, op in enumerate(ops):
            op.idx = i
            op.users = []
            op.nleft = 0
        for op in ops:
            for d in op.deps:
                if d.seg == segi:
                    d.users.append(op)
                    op.nleft += 1
        future = {e: [] for e in engs}
        avail = {e: [] for e in engs}
        free = {e: 0.0 for e in engs}
        for op in ops:
            if op.nleft == 0:
                op.ready = 0.0
                heapq.heappush(future[op.eng], (0.0, op.idx))
        order = {e: [] for e in engs}
        n = 0
        total = len(ops)
        while n < total:
            best = None
            for e in engs:
                T = free[e]
                fu = future[e]
                while fu and fu[0][0] <= T:
                    _, ix = heapq.heappop(fu)
                    heapq.heappush(avail[e], ix)
                if avail[e]:
                    cand = (T, avail[e][0], e, True)
                elif fu:
                    cand = (fu[0][0], fu[0][1], e, False)
                else:
                    continue
                if best is None or cand[:2] < best[:2]:
                    best = cand
            assert best is not None, "scheduler stuck (cyclic deps?)"
            st_, ix, e, from_avail = best
            if from_avail:
                heapq.heappop(avail[e])
            else:
                heapq.heappop(future[e])
            op = ops[ix]
            op.start = st_
            if op.is_dma:
                free[e] = st_ + 70.0
            else:
                free[e] = st_ + op.cost
            op.fin = st_ + op.cost
            order[e].append(op)
            n += 1
            for u in op.users:
                u.nleft -= 1
                if u.nleft == 0:
                    r = 0.0
                    for d in u.deps:
                        if d.seg == segi:
                            lat = 0.0 if (d.eng == u.eng and not d.is_dma) else SYNC_LAT
                            r = max(r, d.fin + lat)
                    u.ready = r
                    heapq.heappush(future[u.eng], (r, u.idx))
        return order

    def emit(self):
        nc = self.nc
        engs = (PE, ACT, DVE, POOL, SP)
        streams = {e: [] for e in engs}
        glob = []
        tbase = 0.0
        for segi, ops in enumerate(self.segments):
            order = self._schedule(ops, segi)
            tmax = 0.0
            for op in ops:
                op.start += tbase
                op.fin += tbase
                tmax = max(tmax, op.fin)
            for e in engs:
                streams[e].extend(order[e])
            glob.extend(sorted(ops, key=lambda o: (o.start, o.idx)))
            if segi < len(self.segments) - 1:
                lasts = [order[e][-1] for e in (PE, ACT, DVE, POOL) if order[e] and not order[e][-1].is_dma]
                dmas = [o for o in ops if o.is_dma]
                for e in engs:
                    b = Op(e, lambda eng: eng.nop(), False, 50.0)
                    b.seg = segi
                    b.idx = len(ops)
                    b.deps = list(lasts) + dmas
                    b.start = tmax
                    b.fin = tmax + 50.0
                    streams[e].append(b)
                    glob.append(b)
            tbase = tmax + 100.0
        self.est_ns = tbase
        for e in engs:
            for i, op in enumerate(streams[e]):
                op.pos = i
        for e in engs:
            for op in streams[e]:
                best = {}
                eff = []
                for d in op.deps:
                    if d.is_dma:
                        eff.append(d)
                        continue
                    if d.eng == PE and op.eng == PE and not op.is_dma:
                        continue
                    if id(d) in op.weak:
                        continue
                    cur = best.get(d.eng)
                    if cur is None or d.pos > cur.pos:
                        best[d.eng] = d
                for d in best.values():
                    d.signal = True
                    eff.append(d)
                op.deps_eff = eff
        done = set()
        ptr = {e: 0 for e in engs}
        progress = True
        while progress:
            progress = False
            for e in engs:
                while ptr[e] < len(streams[e]):
                    op = streams[e][ptr[e]]
                    if all(id(d) in done for d in op.deps):
                        done.add(id(op))
                        ptr[e] += 1
                        progress = True
                    else:
                        break
        assert all(ptr[e] == len(streams[e]) for e in engs), "deadlock in scheduled streams"
        with contextlib.ExitStack() as es:
            esems = {e: es.enter_context(nc.semaphore("s_" + e)) for e in COMPUTE}
            dsems = [es.enter_context(nc.semaphore("d%d" % i)) for i in range(self.n_dma_sems)]
            for e in COMPUTE:
                c = 0
                for op in streams[e]:
                    if op.is_dma:
                        continue
                    if op.signal:
                        c += 1
                        op.sem = esems[e]
                        op.semval = c
            k = 0
            dma_prev = {}
            for op in streams[SP]:
                if op.is_dma:
                    op.signal = True
                    si = k % self.n_dma_sems
                    op.sem = dsems[si]
                    op.semval = 16 * (k // self.n_dma_sems + 1)
                    prev = dma_prev.get(si)
                    if prev is not None:
                        op.deps_eff.append(prev)
                    dma_prev[si] = op
                    k += 1
            finals = list(dma_prev.values())
            block = es.enter_context(nc.Block())

            def run_engine(eng_name):
                def body(eng):
                    waited = {}
                    for op in streams[eng_name]:
                        for d in op.deps_eff:
                            if d.sem is None:
                                continue
                            key = id(d.sem)
                            if waited.get(key, 0) >= d.semval:
                                continue
                            eng.wait_ge(d.sem, d.semval)
                            waited[key] = d.semval
                        ins = op.fn(eng)
                        if op.signal and op.sem is not None:
                            ins.then_inc(op.sem, 16 if op.is_dma else 1)
                    if eng_name == SP:
                        for d in finals:
                            if waited.get(id(d.sem), 0) < d.semval:
                                eng.wait_ge(d.sem, d.semval)
                return body

            block.sync(run_engine(SP))
            block.tensor(run_engine(PE))
            block.scalar(run_engine(ACT))
            block.vector(run_engine(DVE))
            block.gpsimd(run_engine(POOL))


class Tl:
    __slots__ = ("t", "b", "cb")

    def __init__(self, t, name, excl=False):
        self.t = t
        self.b = Buf(name, excl)
        self.cb = None


def build_program(debug=False):
    nc = bass.Bass("TRN2", target_bir_lowering=False)
    P = Prog(nc)

    def din(name, shape, dt=F32):
        return nc.dram_tensor(name, list(shape), dt, kind="ExternalInput").ap()

    def dscr(name, shape, dt):
        return nc.dram_tensor(name, list(shape), dt, kind=("ExternalOutput" if debug else "Internal")).ap()

    x_d = din("xl", [S_LOC, D])
    flag_d = din("flag", [128, 1])
    blkbias_d = din("blkbias", [128, 32])
    pastb_d = din("pastb", [128, 17 * 32])
    om1big_d = din("om1big", [128, 17 * 32])
    esel_d = din("esel", [32, 32 * 128])
    cmask_d = din("cmask", [128, 4 * 512])
    g1T_d = din("g1T", [128, 8])
    g2T_d = din("g2T", [128, 8])
    cw_d = din("cw", [128, 32])
    cb_d = din("cb", [128, 8])
    br_d = din("brT", [128, 8])
    bi_d = din("biT", [128, 8])
    lam_d = din("lamT", [128, 8])
    qg_d = din("qg", [1, 128])
    kg_d = din("kg", [1, 128])
    fcw_d = din("fcw", [128, 66])
    fcb_d = din("fcb", [128, 22])
    win_d = din("w_in", [D, 7168])
    wr_d = din("w_r", [8, 128, 128])
    wi_d = din("w_i", [8, 128, 128])
    pa_d = din("w_pa", [D, D])
    pb_d = din("w_pb", [D, D])
    wo_d = din("w_out", [D, D])
    wu_d = din("w_up", [D, DFF])
    wg_d = din("w_gate", [D, DFF])
    wd_d = din("w_down", [DFF, D])
    out_d = nc.dram_tensor("out", [4096, D], F32, kind="ExternalOutput").ap()

    KT_d = dscr("KT", [8, 128, S_LOC], BF16)
    V_d = dscr("Vs", [S_LOC, D], BF16)
    QT_d = dscr("QT", [8, 128, NQ], BF16)
    MB_d = dscr("MB", [8, 32, NQ], BF16)
    YA_d = dscr("YA", [8, 128, NQ], BF16)
    YB_d = dscr("YB", [8, 128, NQ], BF16)
    XM_d = dscr("XM", [NQ, D], F32)
    HT_d = dscr("HT", [16, 8, 128, 512], BF16)
    B_HT = Buf("HT")
    B_KT, B_V, B_QT, B_MB, B_YA, B_YB, B_XM = (Buf(n) for n in "KT V QT MB YA YB XM".split())

    banks = []
    for i in range(8):
        t = nc.alloc_psum_tensor("bank%d" % i, [128, 512], F32)
        banks.append(Tl(t, "bank%d" % i, excl=True))

    def fsz(ap):
        n = 1
        for d in ap.shape[1:]:
            n *= int(d)
        return n

    def mm(out, lhsT, rhs, start, stop, R, W):
        P.add(PE, lambda e: e.matmul(out, lhsT=lhsT, rhs=rhs, start=start, stop=stop), R, W,
              cost=25.0 + 0.52 * max(fsz(out), 64))

    def tr(out, in_, ident, R, W):
        P.add(PE, lambda e: e.transpose(out, in_, ident), R, W, cost=25.0 + 0.52 * max(fsz(out), 64))

    def act(out, in_, func, R, W, bias=None, scale=1.0, accum=None):
        kw = {}
        if bias is not None:
            kw["bias"] = bias
        if accum is not None:
            kw["accum_out"] = accum
        P.add(ACT, lambda e: e.activation(out=out, in_=in_, func=func, scale=scale, **kw), R, W,
              cost=(224.0 + fsz(out)) / 1.2)

    def vcost(eng, out):
        n = fsz(out)
        if eng == POOL:
            return 300.0 + 2.3 * n
        if eng == ACT:
            return (224.0 + n) / 1.2
        return (130.0 + n) / 0.96

    def tt(eng, out, in0, in1, op, R, W):
        P.add(eng, lambda e: e.tensor_tensor(out=out, in0=in0, in1=in1, op=op), R, W, cost=vcost(eng, out))

    def ts(eng, out, in0, s1, s2, op0, op1, R, W):
        if s2 is None:
            P.add(eng, lambda e: e.tensor_scalar(out=out, in0=in0, scalar1=s1, scalar2=None, op0=op0), R, W,
                  cost=vcost(eng, out))
        else:
            P.add(eng, lambda e: e.tensor_scalar(out=out, in0=in0, scalar1=s1, scalar2=s2, op0=op0, op1=op1), R, W,
                  cost=vcost(eng, out))

    def stt(eng, out, in0, scalar, in1, op0, op1, R, W):
        P.add(eng, lambda e: e.scalar_tensor_tensor(out=out, in0=in0, scalar=scalar, in1=in1, op0=op0, op1=op1), R, W,
              cost=vcost(eng, out))

    def cp(eng, out, in_, R, W):
        if eng == ACT:
            P.add(ACT, lambda e: e.copy(out=out, in_=in_), R, W, cost=vcost(eng, out))
        else:
            P.add(eng, lambda e: e.tensor_copy(out=out, in_=in_), R, W, cost=vcost(eng, out))

    def vop(eng, fn, out, R, W):
        P.add(eng, fn, R, W, cost=vcost(eng, out))

    def memset(eng, ap, val, W):
        P.add(eng, lambda e: e.memset(ap, val), [], W, cost=vcost(eng, ap))

    def dma(out, in_, R=(), W=(), q=None):
        nb = 1
        for d in out.shape:
            nb *= int(d)
        nb *= 2 if out.dtype == BF16 else 4
        return P.dma(SP, out, in_, R, W, nbytes=nb)

    main = contextlib.ExitStack()

    def alloc(stack, name, shape, dt):
        t = stack.enter_context(nc.sbuf_tensor(name, list(shape), dt))
        return Tl(t, name)

    ident = alloc(main, "ident", [128, 128], BF16)
    ones = alloc(main, "ones", [128, 128], BF16)
    eps_t = alloc(main, "eps_t", [128, 1], F32)
    flag = alloc(main, "flag_t", [128, 1], F32)
    g1T = alloc(main, "g1T_t", [128, 8], F32)
    g2T = alloc(main, "g2T_t", [128, 8], F32)
    memset(POOL, ident.t[:], 0.0, [ident.b])
    P.add(POOL, lambda e: e.affine_select(out=ident.t[:], in_=ident.t[:], pattern=[[-1, 128]],
                                          compare_op=ALU.not_equal, fill=1.0, base=0, channel_multiplier=1),
          [ident.b], [ident.b])
    memset(POOL, ones.t[:], 1.0, [ones.b])
    memset(POOL, eps_t.t[:], EPS, [eps_t.b])
    dma(flag.t[:], flag_d[:, :], W=[flag.b])
    dma(g1T.t[:], g1T_d[:, :], W=[g1T.b])
    dma(g2T.t[:], g2T_d[:, :], W=[g2T.b])

    cast_rr = [0]

    def load_weight(stack_tiles, dst, dst_cols0, src, src_col0, ncols, nchunk, scaleT=None, cscale=None, blk=512,
                    by_row=False, defer=None, prio_mod=None):
        stg = stack_tiles
        if prio_mod is None:
            prio_mod = ncols
        if dst.cb is None:
            dst.cb = [Buf() for _ in range(nchunk if by_row else dst.t.shape[-1] // 128)]
        if by_row:
            order = [(c, c0) for c in range(nchunk) for c0 in range(0, ncols, blk)]
        else:
            order = [(c, c0) for c0 in range(0, ncols, blk) for c in range(nchunk)]
        for (c, c0) in order:
            prio = ((c0 % prio_mod) / float(prio_mod)) if not by_row else 2.0 + c / float(nchunk)
            item = (lambda c=c, c0=c0: _load_one(stg, dst, dst_cols0, src, src_col0, ncols, c, c0, scaleT, cscale, blk, by_row))
            if defer is None:
                item()
            else:
                defer.append((prio, len(defer), item))

    def run_deferred(defer):
        for _, _, item in sorted(defer, key=lambda t: (t[0], t[1])):
            item()

    def _load_one(stg, dst, dst_cols0, src, src_col0, ncols, c, c0, scaleT, cscale, blk, by_row):
        if True:
            w = min(blk, ncols - c0)
            s = stg[cast_rr[0] % len(stg)]
            dma(s.t[:, 0:w], src[c * 128:(c + 1) * 128, src_col0 + c0: src_col0 + c0 + w], W=[s.b])
            o = dst.t[:, c, dst_cols0 + c0: dst_cols0 + c0 + w]
            if by_row:
                Wb = [dst.cb[c]]
            else:
                Wb = [dst.cb[k] for k in range((dst_cols0 + c0) // 128, (dst_cols0 + c0 + w + 127) // 128)]
            eng = (DVE, ACT)[cast_rr[0] % 2]
            if scaleT is not None:
                sc = scaleT.t[:, c:c + 1]
                R = [s.b, scaleT.b]
                if eng == ACT:
                    act(o, s.t[:, 0:w], AF.Copy, R, Wb, scale=sc)
                else:
                    ts(DVE, o, s.t[:, 0:w], sc, None, ALU.mult, None, R, Wb)
            elif cscale is not None:
                if eng == ACT:
                    act(o, s.t[:, 0:w], AF.Copy, [s.b], Wb, scale=float(cscale))
                else:
                    ts(DVE, o, s.t[:, 0:w], float(cscale), None, ALU.mult, None, [s.b], Wb)
            else:
                cp(eng, o, s.t[:, 0:w], [s.b], Wb)
            cast_rr[0] += 1

    def wb(wt, col0, ncols=128):
        return [wt.cb[k] for k in range(col0 // 128, (col0 + ncols) // 128)]

    def front_end(st, src_rows_ap_fn, nsub, xt, junk, ssq, lnv, rstd, xn, hT, tpbank, R_src=()):
        for s in range(nsub):
            dma(xt[s].t[:], src_rows_ap_fn(s), R=list(R_src), W=[xt[s].b], q=(SP if s % 2 == 0 else POOL))
        for s in range(nsub):
            act(junk.t[:], xt[s].t[:], AF.Square, [xt[s].b], [junk.b, ssq.b], accum=ssq.t[:, s:s + 1])
        act(lnv.t[:, 0:nsub], ssq.t[:, 0:nsub], AF.Ln, [ssq.b, eps_t.b], [lnv.b], bias=eps_t.t[:], scale=1.0 / D)
        act(rstd.t[:, 0:nsub], lnv.t[:, 0:nsub], AF.Exp, [lnv.b], [rstd.b], scale=-0.5)
        for s in range(nsub):
            ts(DVE, xn[s].t[:], xt[s].t[:], rstd.t[:, s:s + 1], None, ALU.mult, None, [xt[s].b, rstd.b], [xn[s].b])
        for c in range(8):
            bk = tpbank[c % len(tpbank)]
            pv = bk.t[:].bitcast(BF16)
            for s in range(nsub):
                tr(pv[:, 128 * s:128 * (s + 1)], xn[s].t[:, 128 * c:128 * (c + 1)], ident.t[:],
                   [xn[s].b, ident.b], [bk.b])
            cp(ACT if c % 2 == 0 else DVE, hT[c].t[:, 0:128 * nsub], pv[:, 0:128 * nsub], [bk.b], [hT[c].b])

    def front_tiles(st, pfx, nsubmax):
        xt = [alloc(st, pfx + "xt%d" % s, [128, D], F32) for s in range(nsubmax)]
        junk = alloc(st, pfx + "junk", [128, D], BF16)
        ssq = alloc(st, pfx + "ssq", [128, 4], F32)
        lnv = alloc(st, pfx + "lnv", [128, 4], F32)
        rstd = alloc(st, pfx + "rstd", [128, 4], F32)
        xn = [alloc(st, pfx + "xn%d" % s, [128, D], BF16) for s in range(nsubmax)]
        return xt, junk, ssq, lnv, rstd, xn

    with contextlib.ExitStack() as st:
        stg = [alloc(st, "a1stg%d" % i, [128, 512], F32) for i in range(12)]
        Wa = alloc(st, "a1W", [128, 8, 2048], BF16)
        wrb = alloc(st, "a1wr", [128, 8, 128], BF16)
        wib = alloc(st, "a1wi", [128, 8, 128], BF16)
        load_weight(stg, Wa, 0, win_d, 0, 2048, 8, scaleT=g1T)
        for (dst, src) in ((wrb, wr_d), (wib, wi_d)):
            for n in range(8):
                s = stg[cast_rr[0] % len(stg)]
                dma(s.t[:, 0:128], src[n, :, :], W=[s.b])
                cp(DVE, dst.t[:, n, :], s.t[:, 0:128], [s.b], [dst.b])
                cast_rr[0] += 1
        cw = alloc(st, "a1cw", [128, 32], F32)
        cb = alloc(st, "a1cb", [128, 8], F32)
        nbr = alloc(st, "a1nbr", [128, 8], F32)
        nbi = alloc(st, "a1nbi", [128, 8], F32)
        lam = alloc(st, "a1lam", [128, 8], F32)
        c1 = alloc(st, "a1c1", [128, 8], F32)
        c2 = alloc(st, "a1c2", [128, 8], F32)
        dma(cw.t[:], cw_d[:, :], W=[cw.b])
        dma(cb.t[:], cb_d[:, :], W=[cb.b])
        dma(nbr.t[:], br_d[:, :], W=[nbr.b])
        dma(nbi.t[:], bi_d[:, :], W=[nbi.b])
        dma(lam.t[:], lam_d[:, :], W=[lam.b])
        ts(DVE, nbr.t[:], nbr.t[:], -1.0, None, ALU.mult, None, [nbr.b], [nbr.b])
        ts(DVE, nbi.t[:], nbi.t[:], -1.0, None, ALU.mult, None, [nbi.b], [nbi.b])
        act(lam.t[:], lam.t[:], AF.Exp, [lam.b], [lam.b], scale=-1.0)
        act(lam.t[:], lam.t[:], AF.Ln, [lam.b], [lam.b], bias=1.0, scale=1.0)
        ts(DVE, c1.t[:], lam.t[:], -8.0, None, ALU.mult, None, [lam.b], [c1.b])
        ts(DVE, c2.t[:], lam.t[:], -16.0, None, ALU.mult, None, [lam.b], [c2.b])

        xt, junk, ssq, lnv, rstd, xn = front_tiles(st, "a1", 4)
        hT = [[alloc(st, "a1hT%d_%d" % (pb, c), [128, 512], BF16) for c in range(8)] for pb in range(2)]
        xr = [alloc(st, "a1xr%d" % e, [128, 515], F32) for e in range(8)]
        xa = [alloc(st, "a1xa%d" % e, [128, 512], F32) for e in range(3)]
        xab = [alloc(st, "a1xab%d" % e, [128, 512], BF16) for e in range(3)]
        tmp = {n: [alloc(st, "a1%s%d" % (n, i), [128, 512], F32) for i in range(3)]
               for n in ("er", "ei", "a", "a2", "mu", "bx", "hs", "gq", "gu", "ge", "gg")}
        yab = [alloc(st, "a1yab%d" % i, [128, 512], BF16) for i in range(3)]
        hst = alloc(st, "a1hst", [128, 8], F32)
        memset(POOL, hst.t[:], 0.0, [hst.b])
        for e in range(8):
            memset(POOL, xr[e].t[:, 0:3], 0.0, [xr[e].b])
        cnt = 0
        for i in range(16):
            T0 = 512 * i
            full = i >= 7
            pb = i % 2
            front_end(st, lambda s: x_d[T0 + 128 * s: T0 + 128 * (s + 1), :], 4, xt, junk, ssq, lnv, rstd, xn,
                      hT[pb], [banks[0], banks[1]])
            for c in range(8):
                dma(HT_d[i, c, :, :], hT[pb][c].t[:], R=[hT[pb][c].b], W=[B_HT])
            if i == 8:
                ts(DVE, hst.t[:], hst.t[:], flag.t[:, 0:1], None, ALU.mult, None, [hst.b, flag.b], [hst.b])
            for e in range(8):
                j = cnt % 3
                cnt += 1
                bx_, br_, bi_, bg_ = banks[2 + (e % 2)], banks[4], banks[5], banks[6 + (e % 2)]
                for c in range(8):
                    mm(bx_.t[:], Wa.t[:, c, 128 * e:128 * (e + 1)], hT[pb][c].t[:], c == 0, c == 7,
                       wb(Wa, 128 * e) + [hT[pb][c].b], [bx_.b])
                cp(ACT, xr[e].t[:, 3:515], bx_.t[:], [bx_.b], [xr[e].b])
                A = xa[j]
                act(A.t[:], bx_.t[:], AF.Identity, [bx_.b, cw.b, cb.b], [A.b], bias=cb.t[:, e:e + 1],
                    scale=cw.t[:, 4 * e + 3:4 * e + 4])
                for k in range(3):
                    stt(DVE, A.t[:], xr[e].t[:, k:k + 512], cw.t[:, 4 * e + k:4 * e + k + 1], A.t[:], ALU.mult, ALU.add,
                        [xr[e].b, cw.b, A.b], [A.b])
                cp(DVE, xab[j].t[:], A.t[:], [A.b], [xab[j].b])
                cp(POOL, xr[e].t[:, 0:3], xr[e].t[:, 512:515], [xr[e].b], [xr[e].b])
                mm(br_.t[:], wrb.t[:, e, :], xab[j].t[:], True, True, [wrb.b, xab[j].b], [br_.b])
                mm(bi_.t[:], wib.t[:, e, :], xab[j].t[:], True, True, [wib.b, xab[j].b], [bi_.b])
                er, ei, a_, a2, mu, bx, hs = (tmp[n][j] for n in ("er", "ei", "a", "a2", "mu", "bx", "hs"))
                act(er.t[:], br_.t[:], AF.Exp, [br_.b, nbr.b], [er.b], bias=nbr.t[:, e:e + 1], scale=-1.0)
                act(ei.t[:], bi_.t[:], AF.Exp, [bi_.b, nbi.b], [ei.b], bias=nbi.t[:, e:e + 1], scale=-1.0)
                act(er.t[:], er.t[:], AF.Ln, [er.b], [er.b], bias=1.0, scale=1.0)
                act(ei.t[:], ei.t[:], AF.Ln, [ei.b], [ei.b], bias=1.0, scale=1.0)
                act(er.t[:], er.t[:], AF.Exp, [er.b], [er.b], scale=-1.0)
                act(ei.t[:], ei.t[:], AF.Exp, [ei.b], [ei.b], scale=-1.0)
                act(a_.t[:], er.t[:], AF.Exp, [er.b, c1.b], [a_.b], scale=c1.t[:, e:e + 1])
                stt(DVE, a2.t[:], a_.t[:], 0.99999994, a_.t[:], ALU.min, ALU.mult, [a_.b], [a2.b])
                act(mu.t[:], a2.t[:], AF.Ln, [a2.b], [mu.b], bias=1.0, scale=-1.0)
                act(mu.t[:], mu.t[:], AF.Exp, [mu.b], [mu.b], scale=0.5)
                tt(DVE, bx.t[:], ei.t[:], A.t[:], ALU.mult, [ei.b, A.b], [bx.b])
                tt(DVE, bx.t[:], bx.t[:], mu.t[:], ALU.mult, [bx.b, mu.b], [bx.b])
                P.add(DVE, lambda e_, o=hs.t[:], d0=a_.t[:], d1=bx.t[:], ini=hst.t[:, e:e + 1]:
                      e_.tensor_tensor_scan(out=o, data0=d0, data1=d1, initial=ini, op0=ALU.mult, op1=ALU.add),
                      [a_.b, bx.b, hst.b], [hs.b], cost=700.0)
                cp(DVE, hst.t[:, e:e + 1], hs.t[:, 511:512], [hs.b], [hst.b])
                if full:
                    for c in range(8):
                        mm(bg_.t[:], Wa.t[:, c, 1024 + 128 * e:1024 + 128 * (e + 1)], hT[pb][c].t[:], c == 0, c == 7,
                           wb(Wa, 1024 + 128 * e) + [hT[pb][c].b], [bg_.b])
                    gq, gu, ge, gg = (tmp[n][j] for n in ("gq", "gu", "ge", "gg"))
                    ts(DVE, gg.t[:], bg_.t[:], -7.0, None, ALU.max, None, [bg_.b], [gg.b])
                    act(gq.t[:], gg.t[:], AF.Square, [gg.b], [gq.b])
                    ts(DVE, gu.t[:], gq.t[:], 0.044715, 1.0, ALU.mult, ALU.add, [gq.b], [gu.b])
                    tt(DVE, gu.t[:], gu.t[:], gg.t[:], ALU.mult, [gu.b, gg.b], [gu.b])
                    act(ge.t[:], gu.t[:], AF.Exp, [gu.b], [ge.b], scale=-1.5957691216)
                    act(ge.t[:], ge.t[:], AF.Ln, [ge.b], [ge.b], bias=1.0, scale=1.0)
                    act(ge.t[:], ge.t[:], AF.Exp, [ge.b], [ge.b], scale=-1.0)
                    tt(DVE, gg.t[:], ge.t[:], bg_.t[:], ALU.mult, [ge.b, bg_.b], [gg.b])
                    yb_ = yab[(i * 8 + e) % 3]
                    tt(DVE, yb_.t[:], gg.t[:], hs.t[:], ALU.mult, [gg.b, hs.b], [yb_.b])
                    if i == 7:
                        dma(YA_d[e, :, 0:128], yb_.t[:, 384:512], R=[yb_.b], W=[B_YA])
                    else:
                        q0 = 128 + 512 * (i - 8)
                        dma(YA_d[e, :, q0:q0 + 512], yb_.t[:], R=[yb_.b], W=[B_YA])
    P.barrier()

    with contextlib.ExitStack() as st:
        stg = [alloc(st, "a2stg%d" % i, [128, 512], F32) for i in range(12)]
        Wq = alloc(st, "a2W", [128, 8, 3072], BF16)
        load_weight(stg, Wq, 0, win_d, 2048, 3072, 8, scaleT=g1T)
        qgb = alloc(st, "a2qg", [128, 128], F32)
        kgb = alloc(st, "a2kg", [128, 128], F32)
        dma(qgb.t[:], qg_d[0:1, :].broadcast_to([128, 128]), W=[qgb.b])
        dma(kgb.t[:], kg_d[0:1, :].broadcast_to([128, 128]), W=[kgb.b])
        tt(DVE, qgb.t[:], qgb.t[:], kgb.t[:], ALU.mult, [qgb.b, kgb.b], [qgb.b])
        blkb = alloc(st, "a2blkb", [128, 32], F32)
        biaso = alloc(st, "a2biaso", [128, 17, 32], F32)
        valido = alloc(st, "a2valido", [128, 17, 32], F32)
        om1b = alloc(st, "a2om1b", [128, 17, 32], F32)
        dma(blkb.t[:], blkbias_d[:, :], W=[blkb.b])
        dma(biaso.t[:], pastb_d[:, :].rearrange("p (o b) -> p o b", o=17), W=[biaso.b])
        dma(om1b.t[:], om1big_d[:, :].rearrange("p (o b) -> p o b", o=17), W=[om1b.b])
        tt(DVE, biaso.t[:], biaso.t[:], blkb.t[:].rearrange("p (o b) -> p o b", o=1).broadcast_to([128, 17, 32]),
           ALU.add, [biaso.b, blkb.b], [biaso.b])
        ts(DVE, valido.t[:], biaso.t[:], -1e29, None, ALU.is_gt, None, [biaso.b], [valido.b])
        kmT = alloc(st, "a2kmT", [128, 8, 32], F32)
        kmTb = alloc(st, "a2kmTb", [128, 8, 32], BF16)
        memset(POOL, kmT.t[:], 0.0, [kmT.b])
        memset(POOL, kmTb.t[:], 0.0, [kmTb.b])

        xt, junk, ssq, lnv, rstd, xn = front_tiles(st, "a2", 4)
        hT = [[alloc(st, "a2hT%d_%d" % (pb, c), [128, 512], BF16) for c in range(8)] for pb in range(2)]
        raw2 = [[alloc(st, "a2raw%d_%d" % (u, s), [128, D], F32) for s in range(4)] for u in range(2)]
        sq = [alloc(st, "a2sq%d" % s, [128, 512], F32) for s in range(2)]
        ssk2 = [alloc(st, "a2ssk%d" % u, [128, 32], F32) for u in range(2)]
        lnk2 = [alloc(st, "a2lnk%d" % u, [128, 32], F32) for u in range(2)]
        rsk2 = [alloc(st, "a2rsk%d" % u, [128, 32], F32) for u in range(2)]
        nrm2 = [[alloc(st, "a2nrm%d_%d" % (u, s), [128, D], BF16) for s in range(4)] for u in range(2)]
        kTs = [alloc(st, "a2kTs%d" % s, [128, 512], BF16) for s in range(3)]
        qst = [[alloc(st, "a2qst%d_%d" % (u, h), [128, 512], BF16) for h in range(8)] for u in range(2)]
        vb = [alloc(st, "a2vb%d" % s, [128, D], BF16) for s in range(2)]
        gb = alloc(st, "a2gb", [128, 8, 32], F32)
        top8 = alloc(st, "a2top8", [128, 8, 8], F32)
        sel = alloc(st, "a2sel", [128, 8, 32], F32)
        mbq = alloc(st, "a2mbq", [128, 8, 32], BF16)
        mbT = [alloc(st, "a2mbT%d" % s, [32, 8, 128], BF16) for s in range(2)]
        rr = [0]

        def qk_unit(pb, col0, is_q, i):
            u = 1 if is_q else 0
            raw, nrm, ssk, lnk, rsk = raw2[u], nrm2[u], ssk2[u], lnk2[u], rsk2[u]
            for s in range(4):
                for half in range(2):
                    bk = banks[2 + (rr[0] % 4)]
                    rr[0] += 1
                    for c in range(8):
                        mm(bk.t[:], hT[pb][c].t[:, 128 * s:128 * (s + 1)],
                           Wq.t[:, c, col0 + 512 * half: col0 + 512 * (half + 1)], c == 0, c == 7,
                           [hT[pb][c].b] + wb(Wq, col0 + 512 * half, 512), [bk.b])
                    cp(ACT, raw[s].t[:, 512 * half:512 * (half + 1)], bk.t[:], [bk.b], [raw[s].b])
                    sqt = sq[(2 * s + half) % 2]
                    act(sqt.t[:], bk.t[:], AF.Square, [bk.b], [sqt.b])
                    o0 = 8 * s + 4 * half
                    P.add(DVE, lambda e_, o=ssk.t[:, o0:o0 + 4], in_=sqt.t[:].rearrange("p (h d) -> p h d", h=4):
                          e_.tensor_reduce(out=o, in_=in_, axis=AX.X, op=ALU.add), [sqt.b], [ssk.b])
            act(lnk.t[:], ssk.t[:], AF.Ln, [ssk.b, eps_t.b], [lnk.b], bias=eps_t.t[:], scale=1.0 / 128)
            act(rsk.t[:], lnk.t[:], AF.Exp, [lnk.b], [rsk.b], scale=-0.5)
            for s in range(4):
                rv = rsk.t[:, 8 * s:8 * s + 8].rearrange("p (h o) -> p h o", o=1).broadcast_to([128, 8, 128])
                n3 = nrm[s].t[:].rearrange("p (h d) -> p h d", h=8)
                r3 = raw[s].t[:].rearrange("p (h d) -> p h d", h=8)
                if is_q:
                    tt(DVE, r3, r3, rv, ALU.mult, [raw[s].b, rsk.b], [raw[s].b])
                    gv = qgb.t[:].rearrange("p (o d) -> p o d", o=1).broadcast_to([128, 8, 128])
                    tt(DVE, n3, r3, gv, ALU.mult, [raw[s].b, qgb.b], [nrm[s].b])
                else:
                    tt(DVE, n3, r3, rv, ALU.mult, [raw[s].b, rsk.b], [nrm[s].b])

        for i in range(16):
            T0 = 512 * i
            full = i >= 7
            pb = i % 2
            for c in range(8):
                dma(hT[pb][c].t[:], HT_d[i, c, :, :], R=[B_HT], W=[hT[pb][c].b])
            for s in range(4):
                v_ = vb[s % 2]
                for half in range(2):
                    bk = banks[2 + (rr[0] % 4)]
                    rr[0] += 1
                    for c in range(8):
                        mm(bk.t[:], hT[pb][c].t[:, 128 * s:128 * (s + 1)],
                           Wq.t[:, c, 2048 + 512 * half: 2048 + 512 * (half + 1)], c == 0, c == 7,
                           [hT[pb][c].b] + wb(Wq, 2048 + 512 * half, 512), [bk.b])
                    cp(ACT if half == 0 else DVE, v_.t[:, 512 * half:512 * (half + 1)], bk.t[:], [bk.b], [v_.b])
                dma(V_d[T0 + 128 * s:T0 + 128 * (s + 1), :], v_.t[:], R=[v_.b], W=[B_V], q=POOL)
            qk_unit(pb, 1024, False, i)
            for h in range(8):
                bk = banks[6 + (h % 2)]
                pv = bk.t[:].bitcast(BF16)
                for s in range(4):
                    tr(pv[:, 128 * s:128 * (s + 1)], nrm2[0][s].t[:, 128 * h:128 * (h + 1)], ident.t[:],
                       [nrm2[0][s].b, ident.b], [bk.b])
                kt_ = kTs[h % 3]
                cp(ACT, kt_.t[:], pv[:, 0:512], [bk.b], [kt_.b])
                dma(KT_d[h, :, T0:T0 + 512], kt_.t[:], R=[kt_.b], W=[B_KT])
                P.add(DVE, lambda e_, o=kmT.t[:, h, 2 * i:2 * i + 2], in_=kt_.t[:].rearrange("p (b k) -> p b k", b=2):
                      e_.tensor_reduce(out=o, in_=in_, axis=AX.X, op=ALU.add), [kt_.b], [kmT.b])
            ts(DVE, kmTb.t[:, :, 2 * i:2 * i + 2], kmT.t[:, :, 2 * i:2 * i + 2], 1.0 / 256, None, ALU.mult, None,
               [kmT.b], [kmTb.b])
            if not full:
                continue
            qk_unit(pb, 0, True, i)
            qts = []
            for h in range(8):
                bk = banks[6 + (h % 2)]
                pv = bk.t[:].bitcast(BF16)
                for s in range(4):
                    tr(pv[:, 128 * s:128 * (s + 1)], nrm2[1][s].t[:, 128 * h:128 * (h + 1)], ident.t[:],
                       [nrm2[1][s].b, ident.b], [bk.b])
                qs_ = qst[pb][h]
                qt_ap = qs_.t[:]
                cp(ACT if h % 2 == 0 else DVE, qt_ap, pv[:, 0:512], [bk.b], [qs_.b])
                qts.append((qt_ap, qs_.b))
                if i == 7:
                    dma(QT_d[h, :, 0:128], qt_ap[:, 384:512], R=[qs_.b], W=[B_QT])
                else:
                    q0 = 128 + 512 * (i - 8)
                    dma(QT_d[h, :, q0:q0 + 512], qt_ap, R=[qs_.b], W=[B_QT])
            for s in (range(3, 4) if i == 7 else range(4)):
                o = 2 * i + s // 2
                oi = o - 15
                bk = banks[2 + (rr[0] % 4)]
                rr[0] += 1
                g3 = bk.t[:, 0:256].rearrange("p (h b) -> p h b", h=8)
                for h in range(8):
                    mm(bk.t[:, 32 * h:32 * (h + 1)], qts[h][0][:, 128 * s:128 * (s + 1)], kmTb.t[:, h, :], True, True,
                       [qts[h][1], kmTb.b], [bk.b])
                bo = biaso.t[:, oi:oi + 1, :].broadcast_to([128, 8, 32])
                tt(DVE, gb.t[:], g3, bo, ALU.add, [bk.b, biaso.b], [gb.b])
                for h in range(8):
                    P.add(DVE, lambda e_, o_=top8.t[:, h, :], in_=gb.t[:, h, :]: e_.max(out=o_, in_=in_), [gb.b], [top8.b])
                tt(DVE, sel.t[:], gb.t[:], top8.t[:, :, 2:3].broadcast_to([128, 8, 32]), ALU.is_ge, [gb.b, top8.b], [sel.b])
                tt(DVE, sel.t[:], sel.t[:], valido.t[:, oi:oi + 1, :].broadcast_to([128, 8, 32]), ALU.mult,
                   [sel.b, valido.b], [sel.b])
                stt(DVE, mbq.t[:], sel.t[:], BIG, om1b.t[:, oi:oi + 1, :].broadcast_to([128, 8, 32]), ALU.mult, ALU.add,
                    [sel.b, om1b.b], [mbq.b])
                bk2 = banks[6 + (rr[0] % 2)]
                pv2 = bk2.t[:].bitcast(BF16)
                for h in range(8):
                    tr(pv2[0:32, 128 * h:128 * (h + 1)], mbq.t[:, h, :], ident.t[:], [mbq.b, ident.b], [bk2.b])
                m_ = mbT[rr[0] % 2]
                cp(DVE, m_.t[:].rearrange("p h q -> p (h q)"), pv2[0:32, 0:1024], [bk2.b], [m_.b])
                qidx = (T0 + 128 * s) - OWN0
                dma(MB_d[:, :, qidx:qidx + 128].rearrange("h b q -> b h q"), m_.t[:], R=[m_.b], W=[B_MB], q=POOL)
    P.barrier()

    with contextlib.ExitStack() as st:
        esel = alloc(st, "besel", [128, 32, 128], BF16)
        cm = alloc(st, "bcm", [128, 4, 512], BF16)
        stgb = alloc(st, "bstg", [128, 4096], F32)
        memset(DVE, esel.t[:], 0.0, [esel.b])
        dma(stgb.t[0:32, :], esel_d[:, :], W=[stgb.b])
        cp(DVE, esel.t[0:32, :, :].rearrange("p b m -> p (b m)"), stgb.t[0:32, :], [stgb.b, esel.b], [esel.b])
        dma(stgb.t[:, 0:2048], cmask_d[:, :], R=[], W=[stgb.b])
        cp(DVE, cm.t[:].rearrange("p r q -> p (r q)"), stgb.t[:, 0:2048], [stgb.b], [cm.b])
        KTh = [alloc(st, "bKT%d" % i, [128, S_LOC], BF16) for i in range(2)]
        Vh = [alloc(st, "bV%d" % i, [128, 64, 128], BF16) for i in range(2)]
        QTh = [alloc(st, "bQT%d" % i, [128, NQ], BF16) for i in range(2)]
        MBh = [alloc(st, "bMB%d" % i, [128, NQ], BF16) for i in range(2)]
        for i in range(2):
            memset(DVE, MBh[i].t[:], 0.0, [MBh[i].b])
        pt = [alloc(st, "bpt%d" % i, [128, 512], BF16) for i in range(6)]
        dr = [alloc(st, "bdr%d" % i, [128, 512], F32) for i in range(2)]
        yo = [alloc(st, "byo%d" % i, [128, 512], BF16) for i in range(2)]
        dac = [[alloc(st, "bdac%d_%d" % (i, k), [128, 512], F32) for k in range(4)] for i in range(2)]
        dbf = [alloc(st, "bdbf%d" % i, [128, 512], BF16) for i in range(2)]
        cntS = 0
        cntJ = 0
        for h in range(8):
            hb = h % 2
            for q4 in range(4):
                dma(KTh[hb].t[:, 2048 * q4:2048 * (q4 + 1)], KT_d[h, :, 2048 * q4:2048 * (q4 + 1)], R=[B_KT], W=[KTh[hb].b],
                    q=(SP if q4 % 2 == 0 else POOL))
                dma(Vh[hb].t[:, 16 * q4:16 * (q4 + 1), :],
                    V_d[2048 * q4:2048 * (q4 + 1), 128 * h:128 * (h + 1)].rearrange("(n p) d -> p n d", p=128),
                    R=[B_V], W=[Vh[hb].b], q=(POOL if q4 % 2 == 0 else SP))
            dma(QTh[hb].t[:], QT_d[h, :, :], R=[B_QT], W=[QTh[hb].b])
            dma(MBh[hb].t[0:32, :], MB_d[h, :, :], R=[B_MB, MBh[hb].b], W=[MBh[hb].b], q=POOL)
            for j in range(9):
                if j == 0:
                    q0, N, nk, kd0 = 0, 128, 32, 31
                else:
                    q0, N, nk, kd0 = 128 + 512 * (j - 1), 512, 32 + 4 * j, 32 + 4 * (j - 1)
                bo_, bd_ = banks[3 + 2 * (cntJ % 2)], banks[4 + 2 * (cntJ % 2)]
                cntJ += 1
                qv = QTh[hb].t[:, q0:q0 + N]
                mv = MBh[hb].t[:, q0:q0 + N]

                def s_stage(kt):
                    bs = banks[(0, 1, 2, 7)[cntS % 4]]
                    diag = kt >= kd0
                    mm(bs.t[:, 0:N], KTh[hb].t[:, 128 * kt:128 * (kt + 1)], qv, True, False,
                       [KTh[hb].b, QTh[hb].b], [bs.b])
                    mm(bs.t[:, 0:N], esel.t[:, kt // 2, :], mv, False, not diag, [esel.b, MBh[hb].b], [bs.b])
                    if diag:
                        mm(bs.t[:, 0:N], ident.t[:], cm.t[:, kt - kd0, 0:N], False, True, [ident.b, cm.b], [bs.b])
                    p_ = pt[cntS % 6]
                    act(p_.t[:, 0:N], bs.t[:, 0:N], AF.Exp, [bs.b], [p_.b], scale=SCALE)
                    return p_

                daccs = dac[cntJ % 2]

                def o_stage(kt, p_):
                    mm(bo_.t[:, 0:N], Vh[hb].t[:, kt, :], p_.t[:, 0:N], kt == 0, kt == nk - 1, [Vh[hb].b, p_.b], [bo_.b])
                    dacc = daccs[kt % 4]
                    if kt < 4:
                        cp(DVE, dacc.t[:, 0:N], p_.t[:, 0:N], [p_.b], [dacc.b])
                    else:
                        tt(DVE, dacc.t[:, 0:N], dacc.t[:, 0:N], p_.t[:, 0:N], ALU.add, [dacc.b, p_.b], [dacc.b])

                pend = []
                for kt in range(nk):
                    p_ = s_stage(kt)
                    cntS += 1
                    pend.append((kt, p_))
                    if len(pend) > 1:
                        o_stage(*pend.pop(0))
                while pend:
                    o_stage(*pend.pop(0))
                d_ = dr[cntJ % 2]
                y_ = yo[cntJ % 2]
                db_ = dbf[cntJ % 2]
                tt(DVE, daccs[0].t[:, 0:N], daccs[0].t[:, 0:N], daccs[1].t[:, 0:N], ALU.add, [daccs[0].b, daccs[1].b], [daccs[0].b])
                tt(DVE, daccs[2].t[:, 0:N], daccs[2].t[:, 0:N], daccs[3].t[:, 0:N], ALU.add, [daccs[2].b, daccs[3].b], [daccs[2].b])
                tt(DVE, db_.t[:, 0:N], daccs[0].t[:, 0:N], daccs[2].t[:, 0:N], ALU.add, [daccs[0].b, daccs[2].b], [db_.b])
                mm(bd_.t[:, 0:N], ones.t[:], db_.t[:, 0:N], True, True, [ones.b, db_.b], [bd_.b])
                act(d_.t[:, 0:N], bd_.t[:, 0:N], AF.Ln, [bd_.b], [d_.b])
                act(d_.t[:, 0:N], d_.t[:, 0:N], AF.Exp, [d_.b], [d_.b], scale=-1.0)
                tt(DVE, y_.t[:, 0:N], bo_.t[:, 0:N], d_.t[:, 0:N], ALU.mult, [bo_.b, d_.b], [y_.b])
                dma(YB_d[h, :, q0:q0 + N], y_.t[:, 0:N], R=[y_.b], W=[B_YB])
    P.barrier()

    with contextlib.ExitStack() as st:
        stg = [alloc(st, "c1stg%d" % i, [128, 512], F32) for i in range(12)]
        Wg = alloc(st, "c1Wg", [128, 8, 2048], BF16)
        PAw = alloc(st, "c1PA", [128, 8, D], BF16)
        PBw = alloc(st, "c1PB", [128, 8, D], BF16)
        WOw = alloc(st, "c1WO", [128, 8, D], BF16)
        dfr = []
        load_weight(stg, Wg, 0, win_d, 5120, 2048, 8, scaleT=g1T, defer=dfr, prio_mod=1024)
        load_weight(stg, PAw, 0, pa_d, 0, D, 8, defer=dfr)
        load_weight(stg, PBw, 0, pb_d, 0, D, 8, defer=dfr)
        run_deferred(dfr)
        load_weight(stg, WOw, 0, wo_d, 0, D, 8)
        xt, junk, ssq, lnv, rstd, xn = front_tiles(st, "c1", 4)
        hT = [[alloc(st, "c1hT%d_%d" % (pb, c), [128, 512], BF16) for c in range(8)] for pb in range(2)]
        yaT = [alloc(st, "c1ya%d" % pb, [128, 8, 512], BF16) for pb in range(2)]
        ybT = [alloc(st, "c1yb%d" % pb, [128, 8, 512], BF16) for pb in range(2)]
        mT = [alloc(st, "c1mT%d" % c, [128, 512], BF16) for c in range(8)]
        tA = [alloc(st, "c1tA%d" % i, [128, 512], F32) for i in range(2)]
        tB = [alloc(st, "c1tB%d" % i, [128, 512], F32) for i in range(2)]
        xo = [alloc(st, "c1xo%d" % i, [128, D], F32) for i in range(2)]
        rr = 0
        for j in range(9):
            if j == 0:
                q0, N = 0, 128
            else:
                q0, N = 128 + 512 * (j - 1), 512
            nsub = N // 128
            pb = j % 2
            T0 = OWN0 + q0
            for s in range(nsub):
                dma(xt[s].t[:], x_d[T0 + 128 * s: T0 + 128 * (s + 1), :], W=[xt[s].b])
            ti, tc0 = T0 // 512, T0 % 512
            for c in range(8):
                dma(hT[pb][c].t[:, 0:N], HT_d[ti, c, :, tc0:tc0 + N], R=[B_HT], W=[hT[pb][c].b])
            dma(yaT[pb].t[:, :, 0:N], YA_d[:, :, q0:q0 + N].rearrange("c p q -> p c q"), R=[B_YA], W=[yaT[pb].b])
            dma(ybT[pb].t[:, :, 0:N], YB_d[:, :, q0:q0 + N].rearrange("c p q -> p c q"), R=[B_YB], W=[ybT[pb].b], q=POOL)
            for e in range(8):
                bgA, bgB, bpA, bpB = banks[2], banks[3], banks[4], banks[5]
                for c in range(8):
                    mm(bgA.t[:, 0:N], Wg.t[:, c, 128 * e:128 * (e + 1)], hT[pb][c].t[:, 0:N], c == 0, c == 7,
                       wb(Wg, 128 * e) + [hT[pb][c].b], [bgA.b])
                for c in range(8):
                    mm(bgB.t[:, 0:N], Wg.t[:, c, 1024 + 128 * e:1024 + 128 * (e + 1)], hT[pb][c].t[:, 0:N], c == 0, c == 7,
                       wb(Wg, 1024 + 128 * e) + [hT[pb][c].b], [bgB.b])
                for c in range(8):
                    mm(bpA.t[:, 0:N], PAw.t[:, c, 128 * e:128 * (e + 1)], yaT[pb].t[:, c, 0:N], c == 0, c == 7,
                       wb(PAw, 128 * e) + [yaT[pb].b], [bpA.b])
                for c in range(8):
                    mm(bpB.t[:, 0:N], PBw.t[:, c, 128 * e:128 * (e + 1)], ybT[pb].t[:, c, 0:N], c == 0, c == 7,
                       wb(PBw, 128 * e) + [ybT[pb].b], [bpB.b])
                a_, b_ = tA[e % 2], tB[e % 2]
                act(a_.t[:, 0:N], bgA.t[:, 0:N], AF.Exp, [bgA.b], [a_.b], scale=-1.0)
                act(b_.t[:, 0:N], bgB.t[:, 0:N], AF.Exp, [bgB.b], [b_.b], scale=-1.0)
                act(a_.t[:, 0:N], a_.t[:, 0:N], AF.Ln, [a_.b], [a_.b], bias=1.0, scale=1.0)
                act(b_.t[:, 0:N], b_.t[:, 0:N], AF.Ln, [b_.b], [b_.b], bias=1.0, scale=1.0)
                act(a_.t[:, 0:N], a_.t[:, 0:N], AF.Exp, [a_.b], [a_.b], scale=-1.0)
                act(b_.t[:, 0:N], b_.t[:, 0:N], AF.Exp, [b_.b], [b_.b], scale=-1.0)
                tt(DVE, a_.t[:, 0:N], a_.t[:, 0:N], bpA.t[:, 0:N], ALU.mult, [a_.b, bpA.b], [a_.b])
                tt(DVE, b_.t[:, 0:N], b_.t[:, 0:N], bpB.t[:, 0:N], ALU.mult, [b_.b, bpB.b], [b_.b])
                tt(DVE, mT[e].t[:, 0:N], a_.t[:, 0:N], b_.t[:, 0:N], ALU.add, [a_.b, b_.b], [mT[e].b])
            for s in range(nsub):
                xo_ = xo[s % 2]
                for half in range(2):
                    bk = banks[6 + (rr % 2)]
                    rr += 1
                    for e in range(8):
                        mm(bk.t[:], mT[e].t[:, 128 * s:128 * (s + 1)], WOw.t[:, e, 512 * half:512 * (half + 1)],
                           e == 0, e == 7, [mT[e].b] + wb(WOw, 512 * half, 512), [bk.b])
                    tt(DVE, xo_.t[:, 512 * half:512 * (half + 1)], bk.t[:], xt[s].t[:, 512 * half:512 * (half + 1)],
                       ALU.add, [bk.b, xt[s].b], [xo_.b])
                dma(XM_d[q0 + 128 * s:q0 + 128 * (s + 1), :], xo_.t[:], R=[xo_.b], W=[B_XM])
    P.barrier()

    with contextlib.ExitStack() as st:
        NT = 384
        xt, junk, ssq, lnv, rstd, xn = front_tiles(st, "c2", 3)
        tmp = {n: [alloc(st, "c2%s%d" % (n, i), [128, NT], F32) for i in range(3)] for n in ("cv", "sq", "u", "cc")}
        stg = [t for n in ("cv", "sq", "u", "cc") for t in tmp[n]]
        WU = alloc(st, "c2WU", [128, 8, DFF], BF16)
        WG = alloc(st, "c2WG", [128, 8, DFF], BF16)
        WD = alloc(st, "c2WD", [128, NF, D], BF16)
        dfr = []
        load_weight(stg, WU, 0, wu_d, 0, DFF, 8, scaleT=g2T, defer=dfr, blk=384)
        load_weight(stg, WG, 0, wg_d, 0, DFF, 8, scaleT=g2T, defer=dfr, blk=384)
        run_deferred(dfr)
        load_weight(stg, WD, 0, wd_d, 0, D, NF, by_row=True, blk=384)
        fcw = alloc(st, "c2fcw", [128, 66], F32)
        fcb = alloc(st, "c2fcb", [128, 22], F32)
        dma(fcw.t[:], fcw_d[:, :], W=[fcw.b])
        dma(fcb.t[:], fcb_d[:, :], W=[fcb.b])
        hT = [[alloc(st, "c2hT%d_%d" % (pb, c), [128, NT], BF16) for c in range(8)] for pb in range(2)]
        upb = [alloc(st, "c2upb%d" % i, [128, NT + 2], F32) for i in range(3)]
        hbk = [(banks[2 + k].t, banks[2 + k].b) for k in range(6)]
        hal = alloc(st, "c2hal", [128, NF, 2], F32)
        memset(POOL, hal.t[:], 0.0, [hal.b])
        actT = [alloc(st, "c2act%d" % f, [128, NT], BF16) for f in range(NF)]
        rr = 0
        tiles = [(0, 128)] + [(128 + NT * k, NT) for k in range(10)] + [(128 + 3840, 256)]
        for j, (q0, N) in enumerate(tiles):
            nsub = N // 128
            pb = j % 2
            front_end(st, lambda s: XM_d[q0 + 128 * s: q0 + 128 * (s + 1), :], nsub, xt, junk, ssq, lnv, rstd, xn,
                      hT[pb], [banks[0], banks[1]], R_src=[B_XM])
            for f in range(NF):
                (bu_ap, bu_b), (bg_ap, bg_b) = hbk[f % 3], hbk[3 + (f % 3)]
                for c in range(8):
                    mm(bu_ap[:, 0:N], WU.t[:, c, 128 * f:128 * (f + 1)], hT[pb][c].t[:, 0:N], c == 0, c == 7,
                       wb(WU, 128 * f) + [hT[pb][c].b], [bu_b])
                for c in range(8):
                    mm(bg_ap[:, 0:N], WG.t[:, c, 128 * f:128 * (f + 1)], hT[pb][c].t[:, 0:N], c == 0, c == 7,
                       wb(WG, 128 * f) + [hT[pb][c].b], [bg_b])
                u_ = upb[f % 3]
                cv, sq_, uu, gts = (tmp[n][f % 3] for n in ("cv", "sq", "u", "cc"))
                ex = sq_
                cp(ACT, u_.t[:, 0:2], hal.t[:, f, :], [hal.b], [u_.b])
                cp(ACT, u_.t[:, 2:2 + N], bu_ap[:, 0:N], [bu_b], [u_.b])
                act(cv.t[:, 0:N], bu_ap[:, 0:N], AF.Identity, [bu_b, fcw.b, fcb.b], [cv.b], bias=fcb.t[:, f:f + 1],
                    scale=fcw.t[:, 3 * f + 2:3 * f + 3])
                cp(ACT, gts.t[:, 0:N], bg_ap[:, 0:N], [bg_b], [gts.b])
                if j == 0:
                    ts(POOL, hal.t[:, f, :], u_.t[:, N:N + 2], flag.t[:, 0:1], None, ALU.mult, None, [u_.b, flag.b], [hal.b])
                else:
                    cp(POOL, hal.t[:, f, :], u_.t[:, N:N + 2], [u_.b], [hal.b])
                for k in range(2):
                    stt(DVE, cv.t[:, 0:N], u_.t[:, k:k + N], fcw.t[:, 3 * f + k:3 * f + k + 1], cv.t[:, 0:N], ALU.mult, ALU.add,
                        [u_.b, fcw.b, cv.b], [cv.b])
                act(sq_.t[:, 0:N], cv.t[:, 0:N], AF.Square, [cv.b], [sq_.b])
                stt(DVE, uu.t[:, 0:N], sq_.t[:, 0:N], 0.044715, cv.t[:, 0:N], ALU.mult, ALU.mult, [sq_.b, cv.b], [uu.b])
                stt(DVE, uu.t[:, 0:N], uu.t[:, 0:N], 1.0, cv.t[:, 0:N], ALU.mult, ALU.add, [uu.b, cv.b], [uu.b])
                ts(DVE, uu.t[:, 0:N], uu.t[:, 0:N], -22.34, None, ALU.max, None, [uu.b], [uu.b])
                act(ex.t[:, 0:N], uu.t[:, 0:N], AF.Exp, [uu.b], [ex.b], scale=-1.5957691216)
                act(ex.t[:, 0:N], ex.t[:, 0:N], AF.Ln, [ex.b], [ex.b], bias=1.0, scale=1.0)
                act(ex.t[:, 0:N], ex.t[:, 0:N], AF.Exp, [ex.b], [ex.b], scale=-1.0)
                tt(DVE, ex.t[:, 0:N], ex.t[:, 0:N], cv.t[:, 0:N], ALU.mult, [ex.b, cv.b], [ex.b])
                tt(DVE, actT[f].t[:, 0:N], ex.t[:, 0:N], gts.t[:, 0:N], ALU.mult, [ex.b, gts.b], [actT[f].b])
            if j == 0:
                continue
            for s in range(nsub):
                for half in range(2):
                    bk = banks[rr % 2]
                    rr += 1
                    for f in range(NF):
                        mm(bk.t[:], actT[f].t[:, 128 * s:128 * (s + 1)], WD.t[:, f, 512 * half:512 * (half + 1)],
                           f == 0, f == NF - 1, [actT[f].b, WD.cb[f]], [bk.b])
                    tt(DVE, xt[s].t[:, 512 * half:512 * (half + 1)], bk.t[:], xt[s].t[:, 512 * half:512 * (half + 1)],
                       ALU.add, [bk.b, xt[s].b], [xt[s].b])
                r0 = q0 - 128 + 128 * s
                dma(out_d[r0:r0 + 128, :], xt[s].t[:], R=[xt[s].b], W=[])
    P.emit()
    main.close()
    return nc


def _host_consts():
    o = np.arange(15, 32)[:, None]
    b = np.arange(32)[None, :]
    pastb = np.where(b < o, 0.0, -1e30).astype(np.float32)
    om1big = (np.where(b == o, 1.0, 0.0) - 1.0).astype(np.float32) * BIG
    pastb = np.broadcast_to(pastb.reshape(1, -1), (128, 17 * 32)).copy()
    om1big = np.broadcast_to(om1big.reshape(1, -1), (128, 17 * 32)).copy()
    esel = np.zeros((32, 32, 128), np.float32)
    for k in range(32):
        esel[k, k, :] = 1.0
    esel = esel.reshape(32, 32 * 128)
    k = np.arange(128)[:, None, None]
    r = np.arange(4)[None, :, None]
    q = np.arange(512)[None, None, :]
    cmask = np.where(128 * r + k <= q, 0.0, -BIG).astype(np.float32).reshape(128, 4 * 512)
    return pastb, om1big, esel, cmask


def _pc(v, n):
    return np.ascontiguousarray(np.asarray(v, np.float32).reshape(n, 128).T)


_NC_CACHE = {}


def kernel(x, norm1_g, w_in, conv_w, conv_b, w_r, b_r, w_i, b_i, lru_lambda, q_norm_g, k_norm_g,
           w_proj_rnn, w_proj_attn, w_out, norm2_g, w_up, w_gate, ffn_conv_w, ffn_conv_b, w_down):
    x = np.asarray(x, np.float32)
    f = lambda a: np.ascontiguousarray(np.asarray(a, np.float32))
    pastb, om1big, esel, cmask = _host_consts()
    cw = np.asarray(conv_w[0], np.float32)
    cwl = np.ascontiguousarray(cw.T.reshape(8, 128, 4).transpose(1, 0, 2).reshape(128, 32))
    fw = np.asarray(ffn_conv_w[0], np.float32)
    fwl = np.ascontiguousarray(fw.T.reshape(NF, 128, 3).transpose(1, 0, 2).reshape(128, 66))
    common = {
        "pastb": pastb, "om1big": om1big, "esel": esel, "cmask": cmask,
        "g1T": _pc(norm1_g[0], 8), "g2T": _pc(norm2_g[0], 8), "cw": cwl, "cb": _pc(conv_b[0], 8),
        "brT": _pc(np.asarray(b_r[0]).reshape(-1), 8), "biT": _pc(np.asarray(b_i[0]).reshape(-1), 8),
        "lamT": _pc(lru_lambda[0], 8), "qg": f(q_norm_g[0]).reshape(1, 128), "kg": f(k_norm_g[0]).reshape(1, 128),
        "fcw": fwl, "fcb": _pc(ffn_conv_b[0], NF),
        "w_in": f(w_in[0]), "w_r": f(w_r[0]), "w_i": f(w_i[0]), "w_pa": f(w_proj_rnn[0]), "w_pb": f(w_proj_attn[0]),
        "w_out": f(w_out[0]), "w_up": f(w_up[0]), "w_gate": f(w_gate[0]), "w_down": f(w_down[0]),
    }
    in_maps = []
    for core in range(8):
        b, half = core // 2, core % 2
        if half == 0:
            xl = np.zeros((S_LOC, D), np.float32)
            xl[4096:] = x[b, 0:4096]
            flag = np.zeros((128, 1), np.float32)
            blkbias = np.zeros((128, 32), np.float32)
            blkbias[:, 0:16] = -1e30
        else:
            xl = np.ascontiguousarray(x[b])
            flag = np.ones((128, 1), np.float32)
            blkbias = np.zeros((128, 32), np.float32)
        m = dict(common)
        m.update({"xl": xl, "flag": flag, "blkbias": blkbias})
        in_maps.append(m)
    if "nc" not in _NC_CACHE:
        _NC_CACHE["nc"] = build_program()
    res = run_bass_kernel_spmd(_NC_CACHE["nc"], in_maps, core_ids=list(range(8)))
    out = np.empty((4, 8192, D), np.float32)
    for core in range(8):
        b, half = core // 2, core % 2
        out[b, 4096 * half:4096 * (half + 1)] = res.results[core]["out"]
    return out
```
